# Optimizing a Trainium2 kernel written in Bass

```python
import math
import jax, jax.numpy as jnp
from jax import lax
import numpy as np

D_MODEL = 1024
BATCH = 8
SEQ = 4096
DEPTH = 2

D_A = D_MODEL
K_A = 3
D_B = D_MODEL
K_B = 31
N_MEM = 256
N_HEADS = 4
HEAD_DIM = D_MODEL // N_HEADS
D_ATT = N_HEADS * HEAD_DIM
N_BRANCH = 3
D_FF = 2816
K_F = 3
EPS = 1e-6
IN_SPLITS = (3 * D_A, 3 * D_A + 2 * D_B, 3 * D_A + 2 * D_B + D_ATT)
D_IN = 3 * D_A + 2 * D_B + D_ATT + N_BRANCH * D_MODEL

kernel_name = "hybrid_shortconv_conformer_memattn_block"


def _rmsnorm(x, g):
    xf = x.astype(jnp.float32)
    r = lax.rsqrt(jnp.mean(xf * xf, axis=-1, keepdims=True) + EPS)
    return (xf * r).astype(x.dtype) * g


def _layernorm(x, g, b):
    xf = x.astype(jnp.float32)
    mu = jnp.mean(xf, axis=-1, keepdims=True)
    var = jnp.mean(jnp.square(xf - mu), axis=-1, keepdims=True)
    return ((xf - mu) * lax.rsqrt(var + EPS)).astype(x.dtype) * g + b


def _causal_dwconv(x, w):
    k, c = w.shape
    return lax.conv_general_dilated(
        x, w[:, None, :].astype(x.dtype), window_strides=(1,), padding=[(k - 1, 0)],
        dimension_numbers=("NWC", "WIO", "NWC"), feature_group_count=c)


def setup_inputs(seed: int = 0) -> dict:
    key = jax.random.key(seed)
    ks = jax.random.split(key, 24)
    L, D = DEPTH, D_MODEL

    def nrm(k, shape, fan_in):
        return jax.random.normal(k, shape, jnp.float32) * (fan_in ** -0.5)

    def gain(k, shape):
        return 1.0 + 0.02 * jax.random.normal(k, shape, jnp.float32)

    def small(k, shape):
        return 0.02 * jax.random.normal(k, shape, jnp.float32)

    return {
        "x": jax.random.normal(ks[0], (BATCH, SEQ, D), jnp.float32),
        "mem": jax.random.normal(ks[1], (BATCH, N_MEM, D), jnp.float32),
        "norm_mix_g": gain(ks[2], (L, D)),
        "norm_mem_g": gain(ks[3], (L, D)),
        "w_in": nrm(ks[4], (L, D, D_IN), D),
        "b_gate": small(ks[5], (L, N_BRANCH * D)),
        "conv_a_w": nrm(ks[6], (L, K_A, D_A), K_A),
        "w_a_out": nrm(ks[7], (L, D_A, D), D_A),
        "conv_b_w": nrm(ks[8], (L, K_B, D_B), K_B),
        "conv_b_bias": small(ks[9], (L, D_B)),
        "ln_b_g": gain(ks[10], (L, D_B)),
        "ln_b_b": small(ks[11], (L, D_B)),
        "w_b_out": nrm(ks[12], (L, D_B, D), D_B),
        "w_kv": nrm(ks[13], (L, D, 2 * D_ATT), D),
        "w_att_out": nrm(ks[14], (L, D_ATT, D), D_ATT),
        "w_o": nrm(ks[15], (L, D, D), D),
        "norm_ffn_g": gain(ks[16], (L, D)),
        "w_up": nrm(ks[17], (L, D, 2 * D_FF), D),
        "conv_ffn_w": nrm(ks[18], (L, K_F, 2 * D_FF), K_F),
        "w_down": nrm(ks[19], (L, D_FF, D), D_FF),
        "norm_final_g": gain(ks[20], (D,)),
    }


def _mixer(x, mem, norm_mix_g, norm_mem_g, w_in, b_gate, conv_a_w, w_a_out,
           conv_b_w, conv_b_bias, ln_b_g, ln_b_b, w_b_out, w_kv, w_att_out, w_o):
    bsz, seq, _ = x.shape
    h = _rmsnorm(x, norm_mix_g)
    proj = h @ w_in
    p_a, p_b, q, p_g = jnp.split(proj, IN_SPLITS, axis=-1)

    gb, gc, v = jnp.split(p_a, 3, axis=-1)
    y_a = (gb * _causal_dwconv(gc * v, conv_a_w)) @ w_a_out

    u, ug = jnp.split(p_b, 2, axis=-1)
    u = u * jax.nn.sigmoid(ug)
    u = _causal_dwconv(u, conv_b_w) + conv_b_bias
    u = jax.nn.silu(_layernorm(u, ln_b_g, ln_b_b))
    y_b = u @ w_b_out

    memn = _rmsnorm(mem, norm_mem_g)
    k, vm = jnp.split(memn @ w_kv, 2, axis=-1)
    qh = q.reshape(bsz, seq, N_HEADS, HEAD_DIM)
    kh = k.reshape(bsz, N_MEM, N_HEADS, HEAD_DIM)
    vh = vm.reshape(bsz, N_MEM, N_HEADS, HEAD_DIM)
    s = jnp.einsum("bshd,bmhd->bhsm", qh, kh).astype(jnp.float32) * (1.0 / math.sqrt(HEAD_DIM))
    pr = jax.nn.softmax(s, axis=-1).astype(x.dtype)
    o = jnp.einsum("bhsm,bmhd->bshd", pr, vh).reshape(bsz, seq, D_ATT)
    y_c = o @ w_att_out

    g = jax.nn.sigmoid((p_g + b_gate).reshape(bsz, seq, N_BRANCH, D_MODEL))
    merged = g[:, :, 0] * y_a + g[:, :, 1] * y_b + g[:, :, 2] * y_c
    return merged @ w_o


def _conv_ffn(x, norm_ffn_g, w_up, conv_ffn_w, w_down):
    h = _rmsnorm(x, norm_ffn_g)
    u = _causal_dwconv(h @ w_up, conv_ffn_w)
    gt, up = jnp.split(u, 2, axis=-1)
    return (jax.nn.silu(gt) * up) @ w_down


def reference(x, mem, norm_mix_g, norm_mem_g, w_in, b_gate, conv_a_w, w_a_out,
              conv_b_w, conv_b_bias, ln_b_g, ln_b_b, w_b_out, w_kv, w_att_out, w_o,
              norm_ffn_g, w_up, conv_ffn_w, w_down, norm_final_g):
    for l in range(DEPTH):
        x = x + _mixer(x, mem, norm_mix_g[l], norm_mem_g[l], w_in[l], b_gate[l],
                       conv_a_w[l], w_a_out[l], conv_b_w[l], conv_b_bias[l],
                       ln_b_g[l], ln_b_b[l], w_b_out[l], w_kv[l], w_att_out[l], w_o[l])
        x = x + _conv_ffn(x, norm_ffn_g[l], w_up[l], conv_ffn_w[l], w_down[l])
    return _rmsnorm(x, norm_final_g)
```

```python
import os
import numpy as np
import concourse.bass as bass
import concourse.mybir as mybir
from concourse.bass_utils import run_bass_kernel_spmd

F32 = mybir.dt.float32
BF16 = mybir.dt.bfloat16
AF = mybir.ActivationFunctionType
ALU = mybir.AluOpType

P = 128
D = 1024
KC = 8
T = 512
SEQ = 4096
NMEM = 256
DFF = 2816
NJ = 22
NL = 2
DIN = 9216
EPS = 1e-6
SLOT = 6144
NSLOT = 5
NROT = 6
NPRM = 43
NTD = int(os.environ.get("NTD", "8"))


class Trk:
    def __init__(self, nc):
        self.nc = nc
        self.eng = {}
        for name, h in (("pe", nc.tensor), ("act", nc.scalar), ("dve", nc.vector),
                        ("pool", nc.gpsimd), ("sp", nc.sync)):
            self.eng[name] = dict(h=h, sem=nc.alloc_semaphore("s_" + name), cnt=0, seen={})
        self.lastw = {}
        self.readers = {}
        self.dsem = {}

    def _dma_sem(self, name):
        if name not in self.dsem:
            self.dsem[name] = [self.nc.alloc_semaphore("d_" + name), 0]
        return self.dsem[name]

    def emit(self, eng, fn, reads=(), writes=(), dma=None):
        e = self.eng[eng]
        need = {}

        def add(t):
            sname, sem, val, owner = t
            if owner == eng:
                return
            if owner.startswith("dma:"):
                val = self.dsem[owner[4:]][1]
            if e["seen"].get(sname, 0) >= val:
                return
            if sname not in need or need[sname][1] < val:
                need[sname] = (sem, val)

        for k in reads:
            t = self.lastw.get(k)
            if t is not None:
                add(t)
        for k in writes:
            t = self.lastw.get(k)
            if t is not None:
                add(t)
            for t in self.readers.get(k, {}).values():
                add(t)
        waits = list(need.items())
        embed = not (eng == "pool" and dma is None)
        for sname, (sem, val) in (waits[:-1] if embed else waits):
            e["h"].wait_ge(sem, val)
            e["seen"][sname] = val
        insts = fn()
        if not isinstance(insts, (list, tuple)):
            insts = [insts]
        if waits and embed:
            sname, (sem, val) = waits[-1]
            insts[0]._wait_ge(sem, val)
            e["seen"][sname] = val
        if dma is not None:
            ds = self._dma_sem(dma)
            ds[1] += 16
            insts[-1].then_inc(ds[0], 16)
            tk = ("d_" + dma, ds[0], ds[1], "dma:" + dma)
        else:
            e["cnt"] += 1
            insts[-1].then_inc(e["sem"], 1)
            tk = ("s_" + eng, e["sem"], e["cnt"], eng)
        for k in writes:
            self.lastw[k] = tk
            self.readers[k] = {}
        for k in reads:
            if k in writes:
                continue
            r = self.readers.setdefault(k, {})
            old = r.get(tk[0])
            if old is None or old[2] < tk[2]:
                r[tk[0]] = tk
        return tk

    def barrier(self):
        for en, e in self.eng.items():
            for on, o in self.eng.items():
                if on == en or o["cnt"] == 0:
                    continue
                if e["seen"].get("s_" + on, 0) >= o["cnt"]:
                    continue
                e["h"].wait_ge(o["sem"], o["cnt"])
                e["seen"]["s_" + on] = o["cnt"]
            for dn, (sem, cnt) in self.dsem.items():
                if cnt == 0 or e["seen"].get("d_" + dn, 0) >= cnt:
                    continue
                e["h"].wait_ge(sem, cnt)
                e["seen"]["d_" + dn] = cnt

    def wait_all(self, eng, keys):
        e = self.eng[eng]
        for k in keys:
            t = self.lastw.get(k)
            if t is None:
                continue
            sname, sem, val, owner = t
            if e["seen"].get(sname, 0) >= val:
                continue
            e["h"].wait_ge(sem, val)
            e["seen"][sname] = val


def build_program(n_tiles=SEQ // T, depth=NL, do_final=True, debug=False, dbg_t=0):
    nc = bass.Bass("TRN2", target_bir_lowering=False)
    trk = Trk(nc)
    ntok = n_tiles * T

    def din(name, shape):
        return nc.dram_tensor(name, list(shape), F32, kind="ExternalInput").ap()

    x_d = din("x", (ntok, D))
    mem_d = din("mem", (NMEM, D))
    norm_mix_g = din("norm_mix_g", (NL, D))
    norm_mem_g = din("norm_mem_g", (NL, D))
    w_in = din("w_in", (NL, D, DIN))
    b_gate = din("b_gate", (NL, 3 * D))
    conv_a_w = din("conv_a_w", (NL, 3, D))
    w_a_out = din("w_a_out", (NL, D, D))
    conv_b_w = din("conv_b_w", (NL, 31, D))
    conv_b_bias = din("conv_b_bias", (NL, D))
    ln_b_g = din("ln_b_g", (NL, D))
    ln_b_b = din("ln_b_b", (NL, D))
    w_b_out = din("w_b_out", (NL, D, D))
    w_kv = din("w_kv", (NL, D, 2 * D))
    w_att_out = din("w_att_out", (NL, D, D))
    w_o = din("w_o", (NL, D, D))
    norm_ffn_g = din("norm_ffn_g", (NL, D))
    w_up = din("w_up", (NL, D, 2 * DFF))
    conv_ffn_w = din("conv_ffn_w", (NL, 3, 2 * DFF))
    w_down = din("w_down", (NL, DFF, D))
    norm_final_g = din("norm_final_g", (D,))
    y_d = nc.dram_tensor("y", [ntok, D], F32, kind="ExternalOutput").ap()

    def mk_chunks():
        ch = []
        for i in range(4):
            ch.append(("A", i, 6144))
        for c in range(8):
            ch.append(("B", c, 2 * 1024 + 31 * 128))
        for i in range(2):
            ch.append(("Q", i, 4096))
        for i in range(12):
            ch.append(("O", i, 4096))
        for i in range(2):
            ch.append(("WO", i, 4096))
        for jj in range(11):
            ch.append(("F", jj, 2 * 2048))
        for i in range(4):
            ch.append(("D", i, 2 * NJ * 128))
        return ch

    chunks = mk_chunks()
    offs = []
    o = 0
    for (_, _, sz) in chunks:
        assert sz <= SLOT
        offs.append(o)
        o += sz
    TOT = o
    wsc = [nc.dram_tensor(f"wsc{l}", [P, TOT], BF16).ap() for l in range(depth)]

    def sb(name, n, dt):
        return nc.alloc_sbuf_tensor(name, [P, n], dt)

    from contextlib import ExitStack
    ring = [sb(f"ring{i}", SLOT, BF16) for i in range(NSLOT)]
    kT = [sb(f"kT{l}", KC * NMEM, BF16) for l in range(depth)]
    vS = [sb(f"vS{l}", 2 * D, BF16) for l in range(depth)]
    ident = sb("ident", P, F32)
    identb = sb("identb", P, BF16)
    onesb = sb("onesb", P, BF16)
    epsc = sb("epsc", 1, F32)
    prm = sb("prm", KC * P, F32)
    fcw = sb("fcw", 44 * 8, F32)
    hA = sb("hA", NL * 8 * 2, BF16)
    hB = sb("hB", NL * 8 * 30, BF16)
    hF = sb("hF", NL * NJ * 2 * 2, BF16)
    tmpb = [sb(f"tmpb{i}", T, BF16) for i in range(4)]
    rstd = sb("rstd", T, F32)
    xin = [sb(f"xin{i}", D, F32) for i in range(2)]
    setup_stack = ExitStack()

    def sb_tmp(name, n, dt):
        return setup_stack.enter_context(nc.sbuf_tensor(name, [P, n], dt))

    prow = sb_tmp("prow", D, F32)
    frow = sb_tmp("frow", 2 * DFF, F32)
    memT = sb_tmp("memT", KC * NMEM, F32)
    memn = sb_tmp("memn", KC * NMEM, BF16)
    ps = [nc.alloc_psum_tensor(f"ps{i}", [P, T], F32) for i in range(8)]

    E = trk.emit
    tensor, scalar, vector, gpsimd, sync = nc.tensor, nc.scalar, nc.vector, nc.gpsimd, nc.sync

    state = dict(bank=0, tf=0, tb=0, slot=0, xin=0, yout=0, tA=0, gB=0, fGU=0)

    def nbank():
        b = state["bank"]
        state["bank"] = (b + 1) % NROT
        return b

    def ntf():
        i = state["tf"]
        state["tf"] = (i + 1) % len(tmpf)
        return i

    def ntb():
        i = state["tb"]
        state["tb"] = (i + 1) % len(tmpb)
        return i

    def rot(name, n):
        i = state[name]
        state[name] = (i + 1) % n
        return i

    def slot_keys(s):
        return [("slot", s, i) for i in range(8)]

    def mm(bank, pairs, reads, n=T, fine=None):
        if fine is not None:
            for i, (l_ap, r_ap) in enumerate(pairs):
                E("pe", lambda i=i, l_ap=l_ap, r_ap=r_ap: tensor.matmul(
                    ps[bank][:, :n], lhsT=l_ap, rhs=r_ap, start=(i == 0), stop=(i == len(pairs) - 1)),
                  reads=(reads if i == 0 else []) + fine[i], writes=[("ps", bank)])
            return
        def fn():
            out = []
            for i, (l_ap, r_ap) in enumerate(pairs):
                out.append(tensor.matmul(ps[bank][:, :n], lhsT=l_ap, rhs=r_ap,
                                         start=(i == 0), stop=(i == len(pairs) - 1)))
            return out
        return E("pe", fn, reads=reads, writes=[("ps", bank)])

    def prmcol(r, c):
        return prm[:, c * P + r:c * P + r + 1]

    E("pool", lambda: gpsimd.memset(ident[:], 1.0), writes=["ident"])
    E("pool", lambda: gpsimd.affine_select(out=ident[:], in_=ident[:], pattern=[[-1, P]],
                                           compare_op=ALU.is_equal, fill=0.0, base=0, channel_multiplier=1),
      reads=["ident"], writes=["ident"])
    E("dve", lambda: vector.tensor_copy(out=identb[:], in_=ident[:]), reads=["ident"], writes=["identb"])
    E("dve", lambda: vector.memset(onesb[:], 1.0), writes=["onesb"])
    E("dve", lambda: vector.memset(epsc[:], EPS), writes=["epsc"])
    E("dve", lambda: vector.memset(hA[:], 0.0), writes=["hA"])
    E("dve", lambda: vector.memset(hB[:], 0.0), writes=["hB"])
    E("dve", lambda: vector.memset(hF[:], 0.0), writes=["hF"])

    E("dve", lambda: vector.memset(prow[:], 0.0), writes=["prow_all"])

    def prow_dma(row0, nrows, src):
        E("sp", lambda: sync.dma_start(out=prow[row0:row0 + nrows, :], in_=src),
          reads=["prow_all"], writes=[("prow", row0)], dma="prm")

    prow_keys = []
    for l in range(NL):
        base = l * NPRM
        for (r, n, src) in ((0, 1, norm_mix_g[l:l + 1, :]), (1, 1, norm_mem_g[l:l + 1, :]),
                            (2, 3, b_gate[l].rearrange("(i n) -> i n", n=D)),
                            (5, 3, conv_a_w[l]), (8, 31, conv_b_w[l]),
                            (39, 1, conv_b_bias[l:l + 1, :]), (40, 1, ln_b_g[l:l + 1, :]),
                            (41, 1, ln_b_b[l:l + 1, :]), (42, 1, norm_ffn_g[l:l + 1, :])):
            prow_dma(base + r, n, src)
            prow_keys.append(("prow", base + r))
    prow_dma(2 * NPRM, 1, norm_final_g.rearrange("(o n) -> o n", o=1))
    prow_keys.append(("prow", 2 * NPRM))
    NR = P
    E("sp", lambda: sync.dma_start(out=frow[0:6, :], in_=conv_ffn_w.rearrange("l k n -> (l k) n")),
      writes=["frow"], dma="prm")

    for g in range(2):
        b = nbank()

        def fn(g=g, b=b):
            return [tensor.transpose(ps[b][:, i * P:i * P + NR], prow[0:NR, (4 * g + i) * P:(4 * g + i + 1) * P],
                                     ident[0:NR, 0:NR]) for i in range(4)]
        E("pe", fn, reads=prow_keys + ["ident"], writes=[("ps", b)])
        E("dve", lambda g=g, b=b: vector.tensor_copy(
            out=prm[:, 4 * g * P:(4 * g + 4) * P].rearrange("p (a r) -> p a r", r=P)[:, :, 0:NR],
            in_=ps[b][:, :].rearrange("p (a r) -> p a r", r=P)[:, :, 0:NR]),
          reads=[("ps", b)], writes=["prm"])
    for g in range(11):
        b = nbank()

        def fn(g=g, b=b):
            return [tensor.transpose(ps[b][:, i * P:i * P + 6], frow[0:6, (4 * g + i) * P:(4 * g + i + 1) * P],
                                     ident[0:6, 0:6]) for i in range(4)]
        E("pe", fn, reads=["frow", "ident"], writes=[("ps", b)])
        E("dve", lambda g=g, b=b: vector.tensor_copy(
            out=fcw[:, g * 32:(g + 1) * 32].rearrange("p (a r) -> p a r", r=8)[:, :, 0:6],
            in_=ps[b][:, :].rearrange("p (a r) -> p a r", r=P)[:, :, 0:6]),
          reads=[("ps", b)], writes=["fcw"])

    def fcol(l, k, cbk):
        i = cbk * 8 + l * 3 + k
        return fcw[:, i:i + 1]

    def stat_sq(src, n, c, keys):
        ib = ntb()
        E("act", lambda: scalar.activation(out=tmpb[ib][:, :n], in_=src[:, c * n:(c + 1) * n], func=AF.Square),
          reads=keys, writes=[("tmpb", ib)])
        return (c, ib, n)

    def stat_mm(h):
        c, ib, n = h
        E("pe", lambda: tensor.matmul(ps[7][:, :n], lhsT=onesb[:], rhs=tmpb[ib][:, :n],
                                      start=(c == 0), stop=(c == KC - 1)),
          reads=[("tmpb", ib), "onesb"], writes=[("ps", 7)])

    def rstd_finish(n):
        E("act", lambda: scalar.activation(out=rstd[:, :n], in_=ps[7][:, :n], func=AF.Sqrt, scale=1.0 / D,
                                           bias=epsc[:, 0:1]),
          reads=[("ps", 7), "epsc"], writes=["rstd"])
        E("dve", lambda: vector.reciprocal(out=rstd[:, :n], in_=rstd[:, :n]), reads=["rstd"], writes=["rstd"])

    def rms_rstd(src, n, ncols_key_reads):
        for c in range(KC):
            stat_mm(stat_sq(src, n, c, ncols_key_reads(c)))
        rstd_finish(n)

    def fill_w(s, part, dst_off, W2d, col0, ncol, K, sem):
        kc = K // P
        src = W2d[:, col0:col0 + ncol].rearrange("(kc p) c -> p kc c", p=P)
        dst = ring[s][:, dst_off:dst_off + kc * ncol].rearrange("p (kc c) -> p kc c", kc=kc)
        E("pool", lambda: gpsimd.dma_start(out=dst, in_=src), writes=[("slot", s, part)], dma=sem)

    def fill_diag(s, dst_off, col_ap, eng):
        if eng == "dve":
            return vector.tensor_scalar_mul(out=ring[s][:, dst_off:dst_off + P], in0=identb[:], scalar1=col_ap)
        return scalar.activation(out=ring[s][:, dst_off:dst_off + P], in_=identb[:], func=AF.Identity, scale=col_ap)

    mem_v = mem_d.rearrange("(b p) d -> b p d", p=P)
    for blk in range(2):
        xi = rot("xin", 2)
        E("sp", lambda blk=blk, xi=xi: sync.dma_start(out=xin[xi][:], in_=mem_v[blk]),
          writes=[("xin", xi)], dma=f"xin{xi}")
        for hb in range(2):
            b = nbank()

            def fn(blk=blk, xi=xi, hb=hb, b=b):
                return [tensor.transpose(ps[b][:, i * P:(i + 1) * P],
                                         xin[xi][:, (4 * hb + i) * P:(4 * hb + i + 1) * P], ident[:])
                        for i in range(4)]
            E("pe", fn, reads=[("xin", xi), "ident"], writes=[("ps", b)])
            E("dve", lambda blk=blk, hb=hb, b=b: vector.tensor_copy(
                out=memT[:, 4 * hb * NMEM:(4 * hb + 4) * NMEM].rearrange("p (c m) -> p c m", m=NMEM)[:, :, blk * P:(blk + 1) * P],
                in_=ps[b][:, :].rearrange("p (c m) -> p c m", m=P)),
              reads=[("ps", b)], writes=[("memT", blk)])
    rms_rstd(memT, NMEM, lambda c: [("memT", 0), ("memT", 1)])
    for l in range(depth):
        for c in range(KC):
            E("dve", lambda l=l, c=c: vector.scalar_tensor_tensor(
                out=memn[:, c * NMEM:(c + 1) * NMEM], in0=memT[:, c * NMEM:(c + 1) * NMEM],
                scalar=prmcol(l * NPRM + 1, c), in1=rstd[:, :NMEM], op0=ALU.mult, op1=ALU.mult),
              reads=[("memT", 0), ("memT", 1), "prm", "rstd"], writes=["memn"])
        for i in range(2):
            s = rot("slot", NSLOT)
            for j in range(4):
                fill_w(s, j, j * 1024, w_kv[l], (4 * i + j) * P, P, D, f"slot{s}")
            for j in range(4):
                dc = 4 * i + j
                b = nbank()
                mm(b, [(ring[s][:, j * 1024 + kc * P:j * 1024 + (kc + 1) * P], memn[:, kc * NMEM:(kc + 1) * NMEM])
                       for kc in range(KC)], reads=slot_keys(s) + ["memn"], n=NMEM)
                E("act", lambda l=l, dc=dc, b=b: scalar.activation(out=kT[l][:, dc * NMEM:(dc + 1) * NMEM],
                                                                    in_=ps[b][:, :NMEM], func=AF.Copy),
                  reads=[("ps", b)], writes=[("kT", l)])
        for i in range(2):
            s = rot("slot", NSLOT)
            fill_w(s, 0, 0, w_kv[l], D + i * 512, 512, D, f"slot{s}")
            for mb in range(2):
                b = nbank()
                mm(b, [(memn[:, kc * NMEM + mb * P:kc * NMEM + (mb + 1) * P], ring[s][:, kc * 512:(kc + 1) * 512])
                       for kc in range(KC)], reads=slot_keys(s) + ["memn"])
                E("act", lambda l=l, mb=mb, i=i, b=b: scalar.activation(
                    out=vS[l][:, mb * D + i * 512:mb * D + (i + 1) * 512], in_=ps[b][:, :], func=AF.Copy),
                  reads=[("ps", b)], writes=[("vS", l)])

    cidx = {(k, i): n for n, (k, i, _) in enumerate(chunks)}
    dflip = [0]
    for l in range(depth):
        pb = l * NPRM
        for c in range(8):
            s = rot("slot", NSLOT)
            dflip[0] ^= 1
            eng = "dve" if dflip[0] else "act"
            E(eng, lambda s=s, c=c, eng=eng, pb=pb: [fill_diag(s, k * P, prmcol(pb + 8 + k, c), eng)
                                                      for k in range(31)],
              reads=["identb", "prm"], writes=slot_keys(s))
            cbi = cidx[("B", c)]
            E("sp", lambda s=s, l=l, cbi=cbi: sync.dma_start(
                out=wsc[l][:, offs[cbi] + 2048:offs[cbi] + 2048 + 3968], in_=ring[s][:, 0:3968]),
              reads=slot_keys(s), writes=[("wscd", l)], dma=f"wst{s}")

    def fill_diag_pool(s, dst_off, col_ap):
        return gpsimd.tensor_scalar_mul(out=ring[s][:, dst_off:dst_off + P], in0=identb[:], scalar1=col_ap)

    def fill_chunk(s, l, ci):
        pb = l * NPRM
        kind, idx, size = chunks[ci]
        wsize = size
        sem = f"slot{s}"
        if kind == "A":
            for j in range(3):
                fill_w(s, j, j * 2048, w_in[l], j * D + idx * 256, 256, D, sem)
        elif kind == "B":
            c = idx
            for j in range(2):
                fill_w(s, j, j * 1024, w_in[l], (3 + j) * D + c * P, P, D, sem)
            E("sp", lambda: sync.dma_start(out=ring[s][:, 2048:2048 + 3968],
                                           in_=wsc[l][:, offs[ci] + 2048:offs[ci] + 2048 + 3968]),
              reads=[("wscd", l)], writes=[("slot", s, 7)], dma=sem)
            wsize = 2048
        elif kind == "Q":
            fill_w(s, 0, 0, w_in[l], 5 * D + idx * 512, 512, D, sem)
        elif kind == "O":
            pair, bi = idx // 3, idx % 3
            wsrc = (w_a_out, w_b_out, w_att_out)[bi]
            fill_w(s, 0, 0, wsrc[l], pair * 256, 256, D, sem)
            fill_w(s, 1, 2048, w_in[l], (6 + bi) * D + pair * 256, 256, D, sem)
        elif kind == "WO":
            fill_w(s, 0, 0, w_o[l], idx * 512, 512, D, sem)
        elif kind == "F":
            fill_w(s, 0, 0, w_up[l], 2 * idx * P, 256, D, sem)
            fill_w(s, 1, 2048, w_up[l], DFF + 2 * idx * P, 256, D, sem)
        elif kind == "D":
            fill_w(s, 0, 0, w_down[l], 2 * idx * P, 256, DFF, sem)
        if n_tiles > 1:
            E("sp", lambda: sync.dma_start(out=wsc[l][:, offs[ci]:offs[ci] + wsize], in_=ring[s][:, :wsize]),
              reads=slot_keys(s), writes=[("wsc", l, ci)], dma=f"wst{s}")

    trk.barrier()
    setup_stack.close()
    xT = sb("xT", KC * T, F32)
    hT = sb("hT", KC * T, BF16)
    bufA = sb("bufA", 24 * T, BF16)
    qm = sb("qm", KC * T, BF16)
    cb = sb("cb", KC * T, F32)
    tA = [sb(f"tA{i}", 2 + T, BF16) for i in range(2)]
    gB = [sb(f"gB{i}", 30 + T, BF16) for i in range(2)]
    fGU = [sb(f"fGU{i}", 2 * (2 + T), BF16) for i in range(2)]
    tmpf = [sb(f"tmpf{i}", T, F32) for i in range(6)]
    pT = [sb(f"pT{i}", T, BF16) for i in range(4)]
    mean = sb("mean", T, F32)
    lnr = sb("lnr", T, F32)
    yout = [sb(f"yout{i}", D, F32) for i in range(2)]
    class Stream:
        def __init__(self):
            self.seq = [(t, l, ci) for t in range(n_tiles) for l in range(depth) for ci in range(len(chunks))]
            self.nload = 0
            self.ncons = 0
            self.slots = {}
            self.held = set()

        def load_next(self):
            if self.nload >= len(self.seq):
                return
            victim = self.nload - NSLOT
            assert victim < self.ncons and victim not in self.held, (victim, self.ncons, self.held)
            (t, l, ci) = self.seq[self.nload]
            s = rot("slot", NSLOT)
            size = chunks[ci][2]
            if t == 0:
                fill_chunk(s, l, ci)
            else:
                E("sp", lambda: sync.dma_start(out=ring[s][:, :size], in_=wsc[l][:, offs[ci]:offs[ci] + size]),
                  reads=[("wsc", l, ci)], writes=slot_keys(s), dma=f"slot{s}")
            self.slots[self.nload] = s
            self.nload += 1

        def get(self, t, l, ci, hold=False):
            assert self.seq[self.ncons] == (t, l, ci), (self.seq[self.ncons], (t, l, ci))
            while self.nload < min(self.ncons + NSLOT - 1, len(self.seq)):
                self.load_next()
            n = self.ncons
            s = self.slots.pop(n)
            if hold:
                self.held.add(n)
            self.ncons += 1
            return s, n

        def unhold(self, n):
            self.held.discard(n)

    stream = Stream()
    x_v = x_d.rearrange("(n p) d -> n p d", p=P)
    y_v = y_d.rearrange("(n p) d -> n p d", p=P)
    xkeys = [("xT", c) for c in range(KC)]
    hkeys = [("hT", c) for c in range(KC)]

    xpref = {}

    def x_dma(t, blk):
        xi = rot("xin", 2)
        E("sp", lambda: sync.dma_start(out=xin[xi][:], in_=x_v[4 * t + blk]),
          writes=[("xin", xi)], dma=f"xin{xi}")
        return xi

    def load_x_tile(t):
        for blk in range(4):
            xi = xpref.pop((t, blk)) if (t, blk) in xpref else x_dma(t, blk)
            for hb in range(2):
                b = nbank()

                def fn(xi=xi, hb=hb, b=b):
                    return [tensor.transpose(ps[b][:, i * P:(i + 1) * P],
                                             xin[xi][:, (4 * hb + i) * P:(4 * hb + i + 1) * P], ident[:])
                            for i in range(4)]
                E("pe", fn, reads=[("xin", xi), "ident"], writes=[("ps", b)])
                E("act", lambda blk=blk, hb=hb, b=b: scalar.activation(
                    out=xT[:, 4 * hb * T:(4 * hb + 4) * T].rearrange("p (c m) -> p c m", m=T)[:, :, blk * P:(blk + 1) * P],
                    in_=ps[b][:, :].rearrange("p (c m) -> p c m", m=P), func=AF.Copy),
                  reads=[("ps", b)], writes=[("xT", 4 * hb + i) for i in range(4)])
        if t + 1 < n_tiles:
            for blk in range(2):
                xpref[(t + 1, blk)] = x_dma(t + 1, blk)

    def scale_rows(dst, dkey, grow):
        for c in range(KC):
            eng, h = "dve", vector
            E(eng, lambda c=c, h=h: h.scalar_tensor_tensor(
                out=dst[:, c * T:(c + 1) * T], in0=xT[:, c * T:(c + 1) * T], scalar=prmcol(grow, c),
                in1=rstd[:, :], op0=ALU.mult, op1=ALU.mult),
              reads=[("xT", c), "rstd", "prm"], writes=[(dkey, c)])

    def norm_to_h(grow, have_stats):
        if have_stats:
            rstd_finish(T)
        else:
            rms_rstd(xT, T, lambda c: [("xT", c)])
        scale_rows(hT, "hT", grow)

    def resid_phase(groups):
        pend = None
        for dc, grp in enumerate(groups):
            pairs, reads = grp[0], grp[1]
            b = nbank()
            mm(b, pairs, reads, fine=(grp[2] if len(grp) > 2 else None))
            E("dve", lambda dc=dc, b=b: vector.tensor_tensor(out=xT[:, dc * T:(dc + 1) * T],
                                                              in0=xT[:, dc * T:(dc + 1) * T], in1=ps[b][:, :],
                                                              op=ALU.add),
              reads=[("ps", b), ("xT", dc)], writes=[("xT", dc)])
            h = stat_sq(xT, T, dc, [("xT", dc)])
            if pend is not None:
                stat_mm(pend)
            pend = h
        stat_mm(pend)

    def hrhs(kc):
        return hT[:, kc * T:(kc + 1) * T]

    def wblk(s, off, kc):
        return ring[s][:, off + kc * P:off + (kc + 1) * P]

    ZA, ZB, ZC = 0, 8, 16

    def zblk(i):
        return bufA[:, i * T:(i + 1) * T]

    def mixer(t, l):
        pb = l * NPRM
        D0 = (t == dbg_t and l == 0)
        if D0:
            dbg("xT", xT, KC * T, xkeys)
        norm_to_h(pb + 0, have_stats=(l > 0))
        if D0:
            dbg("hT", hT, KC * T, hkeys)
        cbase = 0
        def a_pair(pair):
            s, _ = stream.get(t, l, cbase + pair)
            sk = slot_keys(s)
            for cc in range(2):
                c = 2 * pair + cc
                bg, bc, bv = nbank(), nbank(), nbank()
                for j, b in enumerate((bg, bc, bv)):
                    pairs = [(ring[s][:, j * 2048 + kc * 256 + cc * P:j * 2048 + kc * 256 + (cc + 1) * P], hrhs(kc))
                             for kc in range(KC)]
                    if c == 0 and j == 0:
                        mm(b, pairs, reads=sk, fine=[[("hT", kc)] for kc in range(KC)])
                    else:
                        mm(b, pairs, reads=sk + hkeys)
                ig, ic = ntf(), ntf()
                E("act", lambda b=bc, i=ic: scalar.activation(out=tmpf[i][:], in_=ps[b][:, :], func=AF.Copy),
                  reads=[("ps", bc)], writes=[("tmpf", ic)])
                E("act", lambda b=bg, i=ig: scalar.activation(out=tmpf[i][:], in_=ps[b][:, :], func=AF.Copy),
                  reads=[("ps", bg)], writes=[("tmpf", ig)])
                ia = rot("tA", 2)
                hofs = (l * 8 + c) * 2
                E("dve", lambda ia=ia, hofs=hofs: vector.tensor_copy(out=tA[ia][:, 0:2], in_=hA[:, hofs:hofs + 2]),
                  reads=["hA"], writes=[("tA", ia)])
                E("dve", lambda ia=ia, ic=ic, bv=bv: vector.tensor_tensor(out=tA[ia][:, 2:2 + T], in0=ps[bv][:, :],
                                                                           in1=tmpf[ic][:], op=ALU.mult),
                  reads=[("ps", bv), ("tmpf", ic), ("tA", ia)], writes=[("tA", ia)])
                E("dve", lambda ic=ic, bv=bv, hofs=hofs: vector.tensor_tensor(
                    out=hA[:, hofs:hofs + 2], in0=ps[bv][:, T - 2:T], in1=tmpf[ic][:, T - 2:T], op=ALU.mult),
                  reads=[("ps", bv), ("tmpf", ic)], writes=["hA"])
                iacc = ntf()
                E("dve", lambda ia=ia, iacc=iacc, c=c: vector.tensor_scalar_mul(
                    out=tmpf[iacc][:], in0=tA[ia][:, 2:2 + T], scalar1=prmcol(pb + 5 + 2, c)),
                  reads=[("tA", ia), "prm"], writes=[("tmpf", iacc)])
                for k in (1, 0):
                    E("dve", lambda ia=ia, iacc=iacc, c=c, k=k: vector.scalar_tensor_tensor(
                        out=tmpf[iacc][:], in0=tA[ia][:, k:k + T], scalar=prmcol(pb + 5 + k, c),
                        in1=tmpf[iacc][:], op0=ALU.mult, op1=ALU.add),
                      reads=[("tA", ia), ("tmpf", iacc), "prm"], writes=[("tmpf", iacc)])
                E("dve", lambda c=c, iacc=iacc, ig=ig: vector.tensor_tensor(out=zblk(ZA + c), in0=tmpf[iacc][:],
                                                                            in1=tmpf[ig][:], op=ALU.mult),
                  reads=[("tmpf", iacc), ("tmpf", ig)], writes=[("z", ZA + c)])

        S1, S2 = 6, 7

        def b_part1(c):
            s, sn = stream.get(t, l, cbase + 4 + c, hold=True)
            sk = slot_keys(s)
            bu, bug = nbank(), nbank()
            mm(bu, [(wblk(s, 0, kc), hrhs(kc)) for kc in range(KC)], reads=sk + hkeys)
            mm(bug, [(wblk(s, 1024, kc), hrhs(kc)) for kc in range(KC)], reads=sk + hkeys)
            isg = ntf()
            E("act", lambda b=bug, i=isg: scalar.activation(out=tmpf[i][:], in_=ps[b][:, :], func=AF.Sigmoid),
              reads=[("ps", bug)], writes=[("tmpf", isg)])
            ig = rot("gB", 2)
            hofs = (l * 8 + c) * 30
            E("dve", lambda ig=ig, hofs=hofs: vector.tensor_copy(out=gB[ig][:, 0:30], in_=hB[:, hofs:hofs + 30]),
              reads=["hB"], writes=[("gB", ig)])
            E("dve", lambda ig=ig, isg=isg, bu=bu: vector.tensor_tensor(out=gB[ig][:, 30:30 + T], in0=ps[bu][:, :],
                                                                         in1=tmpf[isg][:], op=ALU.mult),
              reads=[("ps", bu), ("tmpf", isg), ("gB", ig)], writes=[("gB", ig)])
            E("dve", lambda isg=isg, bu=bu, hofs=hofs: vector.tensor_tensor(
                out=hB[:, hofs:hofs + 30], in0=ps[bu][:, T - 30:T], in1=tmpf[isg][:, T - 30:T], op=ALU.mult),
              reads=[("ps", bu), ("tmpf", isg)], writes=["hB"])
            return (c, s, sk, ig, sn)

        def b_part2(c, s, sk, ig, sn):
            bo = nbank()
            npe = 31 - NTD
            mm(bo, [(ring[s][:, 2048 + k * P:2048 + (k + 1) * P], gB[ig][:, k:k + T]) for k in range(npe)],
               reads=sk + [("gB", ig)])
            stream.unhold(sn)
            if NTD > 0:
                iacc = ntf()
                for k in range(npe, 31):
                    E("dve", lambda k=k, iacc=iacc, first=(k == npe): vector.scalar_tensor_tensor(
                        out=tmpf[iacc][:], in0=gB[ig][:, k:k + T], scalar=prmcol(pb + 8 + k, c),
                        in1=(ps[bo][:, :] if first else tmpf[iacc][:]), op0=ALU.mult, op1=ALU.add),
                      reads=[("gB", ig), "prm"] + ([("ps", bo)] if k == npe else [("tmpf", iacc)]),
                      writes=[("tmpf", iacc)])
                csrc, ckey = tmpf[iacc][:], ("tmpf", iacc)
            else:
                csrc, ckey = ps[bo][:, :], ("ps", bo)
            E("act", lambda c=c: scalar.activation(out=cb[:, c * T:(c + 1) * T], in_=csrc,
                                                   func=AF.Identity, bias=prmcol(pb + 39, c)),
              reads=[ckey, "prm"], writes=[("cb", c)])
            i1, i2 = ntb(), ntb()
            E("act", lambda c=c, i2=i2: scalar.activation(out=tmpb[i2][:], in_=csrc,
                                                          func=AF.Square, bias=prmcol(pb + 39, c)),
              reads=[ckey, "prm"], writes=[("tmpb", i2)])
            E("dve", lambda c=c, i1=i1: vector.tensor_copy(out=tmpb[i1][:], in_=cb[:, c * T:(c + 1) * T]),
              reads=[("cb", c)], writes=[("tmpb", i1)])
            return (c, i1, i2)

        def b_part3(c, i1, i2):
            E("pe", lambda c=c, i1=i1: tensor.matmul(ps[S1][:, :], lhsT=onesb[:], rhs=tmpb[i1][:],
                                                      start=(c == 0), stop=(c == 7)),
              reads=[("tmpb", i1), "onesb"], writes=[("ps", S1)])
            E("pe", lambda c=c, i2=i2: tensor.matmul(ps[S2][:, :], lhsT=onesb[:], rhs=tmpb[i2][:],
                                                      start=(c == 0), stop=(c == 7)),
              reads=[("tmpb", i2), "onesb"], writes=[("ps", S2)])

        for pair in range(4):
            a_pair(pair)
        pb1 = None
        pb2 = None
        for c in range(8):
            cur = b_part1(c)
            nxt2 = None
            if pb1 is not None:
                nxt2 = b_part2(*pb1)
            if pb2 is not None:
                b_part3(*pb2)
            pb1, pb2 = cur, nxt2
        last2 = b_part2(*pb1)
        cbase += 12

        def b_tail_stats():
            if pb2 is not None:
                b_part3(*pb2)
            b_part3(*last2)

        def ln_chain():
            E("dve", lambda: vector.tensor_scalar_mul(out=mean[:], in0=ps[S1][:, :], scalar1=1.0 / D),
              reads=[("ps", S1)], writes=["mean"])
            im = ntf()
            E("dve", lambda im=im: vector.tensor_tensor(out=tmpf[im][:], in0=mean[:], in1=mean[:], op=ALU.mult),
              reads=["mean"], writes=[("tmpf", im)])
            E("dve", lambda im=im: vector.scalar_tensor_tensor(out=lnr[:], in0=ps[S2][:, :], scalar=1.0 / D,
                                                                in1=tmpf[im][:], op0=ALU.mult, op1=ALU.subtract),
              reads=[("ps", S2), ("tmpf", im)], writes=["lnr"])
            E("act", lambda: scalar.activation(out=lnr[:], in_=lnr[:], func=AF.Sqrt, bias=epsc[:, 0:1]),
              reads=["lnr", "epsc"], writes=["lnr"])
            E("dve", lambda: vector.reciprocal(out=lnr[:], in_=lnr[:]), reads=["lnr"], writes=["lnr"])
            for c in range(8):
                E("pool", lambda c=c: gpsimd.tensor_tensor(out=cb[:, c * T:(c + 1) * T], in0=cb[:, c * T:(c + 1) * T],
                                                           in1=mean[:], op=ALU.subtract),
                  reads=[("cb", c), "mean"], writes=[("cb", c)])
                E("pool", lambda c=c: gpsimd.tensor_tensor(out=cb[:, c * T:(c + 1) * T], in0=cb[:, c * T:(c + 1) * T],
                                                           in1=lnr[:], op=ALU.mult),
                  reads=[("cb", c), "lnr"], writes=[("cb", c)])

        def ln_silu():
            for c in range(8):
                E("act", lambda c=c: scalar.activation(out=zblk(ZB + c), in_=cb[:, c * T:(c + 1) * T], func=AF.Silu,
                                                       bias=prmcol(pb + 41, c), scale=prmcol(pb + 40, c)),
                  reads=[("cb", c), "prm"], writes=[("z", ZB + c)])

        for i in range(2):
            s, _ = stream.get(t, l, cbase + i)
            sk = slot_keys(s)
            for j in range(4):
                dc = 4 * i + j
                b = nbank()
                mm(b, [(ring[s][:, kc * 512 + j * P:kc * 512 + (j + 1) * P], hrhs(kc)) for kc in range(KC)],
                   reads=sk + hkeys)
                E("act", lambda dc=dc, b=b: scalar.activation(out=qm[:, dc * T:(dc + 1) * T], in_=ps[b][:, :],
                                                               func=AF.Copy),
                  reads=[("ps", b)], writes=[("qm", dc)])
                if dc == 0:
                    b_tail_stats()
                if dc == 1:
                    ln_chain()
        cbase += 2
        def att_s(hd):
            for mb in range(2):
                b = nbank()
                ip = 2 * (hd % 2) + mb
                mm(b, [(kT[l][:, (2 * hd + kk) * NMEM + mb * P:(2 * hd + kk) * NMEM + (mb + 1) * P],
                        qm[:, (2 * hd + kk) * T:(2 * hd + kk + 1) * T]) for kk in range(2)],
                   reads=[("kT", l), ("qm", 2 * hd), ("qm", 2 * hd + 1)])
                E("act", lambda ip=ip, b=b: scalar.activation(out=pT[ip][:], in_=ps[b][:, :], func=AF.Exp,
                                                               scale=1.0 / 16.0),
                  reads=[("ps", b)], writes=[("pT", ip)])

        def att_o(hd):
            ips = [2 * (hd % 2), 2 * (hd % 2) + 1]
            pk = [("pT", ip) for ip in ips]
            bd = nbank()
            mm(bd, [(onesb[:], pT[ip][:]) for ip in ips], reads=pk + ["onesb"])
            ir = ntf()
            E("dve", lambda ir=ir, bd=bd: vector.reciprocal(out=tmpf[ir][:], in_=ps[bd][:, :]),
              reads=[("ps", bd)], writes=[("tmpf", ir)])
            for dd in range(2):
                dc = 2 * hd + dd
                b = nbank()
                mm(b, [(vS[l][:, mb * D + dc * P:mb * D + (dc + 1) * P], pT[ips[mb]][:]) for mb in range(2)],
                   reads=pk + [("vS", l)])
                E("dve", lambda dc=dc, b=b, ir=ir: vector.tensor_tensor(out=zblk(ZC + dc), in0=ps[b][:, :],
                                                                         in1=tmpf[ir][:], op=ALU.mult),
                  reads=[("ps", b), ("tmpf", ir)], writes=[("z", ZC + dc)])

        if os.environ.get("ATT_PIPE", "1") == "1":
            att_s(0)
            for hd in range(4):
                if hd + 1 < 4:
                    att_s(hd + 1)
                att_o(hd)
        else:
            for hd in range(4):
                att_s(hd)
                att_o(hd)
        ln_silu()
        if D0:
            dbg("zabc", bufA, 24 * T, [("z", i) for i in range(24)])
            dbg("qT", qm, KC * T, [("qm", i) for i in range(8)])
            dbg("cb", cb, KC * T, [("cb", i) for i in range(8)])
        zk = {ZA: [("z", ZA + i) for i in range(8)], ZB: [("z", ZB + i) for i in range(8)],
              ZC: [("z", ZC + i) for i in range(8)]}
        for pair in range(4):
            accs = [None, None]
            for bi, zb in enumerate((ZA, ZB, ZC)):
                s, _ = stream.get(t, l, cbase + pair * 3 + bi)
                sk = slot_keys(s)
                for cc in range(2):
                    c = 2 * pair + cc
                    by, bgt = nbank(), nbank()
                    mm(by, [(ring[s][:, kc * 256 + cc * P:kc * 256 + (cc + 1) * P], zblk(zb + kc))
                            for kc in range(KC)], reads=sk + zk[zb])
                    mm(bgt, [(ring[s][:, 2048 + kc * 256 + cc * P:2048 + kc * 256 + (cc + 1) * P], hrhs(kc))
                             for kc in range(KC)], reads=sk + hkeys)
                    igt = ntf()
                    E("act", lambda bgt=bgt, igt=igt, bi=bi, c=c: scalar.activation(
                        out=tmpf[igt][:], in_=ps[bgt][:, :], func=AF.Sigmoid, bias=prmcol(pb + 2 + bi, c)),
                      reads=[("ps", bgt), "prm"], writes=[("tmpf", igt)])
                    E("dve", lambda by=by, igt=igt: vector.tensor_tensor(out=tmpf[igt][:], in0=ps[by][:, :],
                                                                          in1=tmpf[igt][:], op=ALU.mult),
                      reads=[("ps", by), ("tmpf", igt)], writes=[("tmpf", igt)])
                    if bi == 0:
                        accs[cc] = igt
                    elif bi == 1:
                        acc = accs[cc]
                        E("dve", lambda acc=acc, igt=igt: vector.tensor_tensor(out=tmpf[acc][:], in0=tmpf[acc][:],
                                                                                in1=tmpf[igt][:], op=ALU.add),
                          reads=[("tmpf", acc), ("tmpf", igt)], writes=[("tmpf", acc)])
                    else:
                        acc = accs[cc]
                        E("dve", lambda acc=acc, igt=igt, c=c: vector.tensor_tensor(
                            out=qm[:, c * T:(c + 1) * T], in0=tmpf[acc][:], in1=tmpf[igt][:], op=ALU.add),
                          reads=[("tmpf", acc), ("tmpf", igt)], writes=[("qm", c)])
        cbase += 12
        qk = [("qm", i) for i in range(8)]
        if D0:
            dbg("mT", qm, KC * T, qk)
        def wo_groups():
            for i in range(2):
                s, _ = stream.get(t, l, cbase + i)
                sk = slot_keys(s)
                for j in range(4):
                    yield ([(ring[s][:, kc * 512 + j * P:kc * 512 + (j + 1) * P], qm[:, kc * T:(kc + 1) * T])
                            for kc in range(KC)], sk + qk)
        resid_phase(wo_groups())
        cbase += 2
        if D0:
            dbg("x1", xT, KC * T, xkeys)
        return cbase

    def ffn(t, l, cbase):
        pb = l * NPRM
        norm_to_h(pb + 42, have_stats=True)
        W2 = 2 + T
        for jj in range(11):
            s, _ = stream.get(t, l, cbase + jj)
            sk = slot_keys(s)
            for j2 in range(2):
                j = 2 * jj + j2
                o0 = j2 * 2048
                bgt, bup = nbank(), nbank()
                gtp = [(ring[s][:, kc * 256 + j2 * P:kc * 256 + (j2 + 1) * P], hrhs(kc)) for kc in range(KC)]
                upp = [(ring[s][:, 2048 + kc * 256 + j2 * P:2048 + kc * 256 + (j2 + 1) * P], hrhs(kc))
                       for kc in range(KC)]
                if j == 0:
                    mm(bgt, gtp, reads=sk, fine=[[("hT", kc)] for kc in range(KC)])
                else:
                    mm(bgt, gtp, reads=sk + hkeys)
                mm(bup, upp, reads=sk + hkeys)
                fi = rot("fGU", 2)
                hofs = (l * NJ + j) * 4
                fv = fGU[fi][:, :].rearrange("p (a w) -> p a w", w=W2)
                if False:
                    pass
                else:
                    E("act", lambda fi=fi, hofs=hofs, fv=fv: scalar.activation(
                        out=fv[:, :, 0:2], in_=hF[:, hofs:hofs + 4].rearrange("p (a w) -> p a w", w=2), func=AF.Copy),
                      reads=["hF"], writes=[("fGh", fi)])
                accs = []
                for hh, (bk, cbk) in enumerate(((bgt, j), (bup, NJ + j))):
                    ia = ntf()
                    accs.append(ia)
                    base = hh * W2
                    if hh == 0 and os.environ.get("F_DVECOPY", "0") == "1":
                        E("dve", lambda fi=fi, bk=bk, base=base: vector.tensor_copy(
                            out=fGU[fi][:, base + 2:base + 2 + T], in_=ps[bk][:, :]),
                          reads=[("ps", bk)], writes=[("fGm", fi, hh)])
                    else:
                        E("act", lambda fi=fi, bk=bk, base=base: scalar.activation(
                            out=fGU[fi][:, base + 2:base + 2 + T], in_=ps[bk][:, :], func=AF.Copy),
                          reads=[("ps", bk)], writes=[("fGm", fi, hh)])
                    E("act", lambda bk=bk, ia=ia, cbk=cbk: scalar.activation(
                        out=tmpf[ia][:], in_=ps[bk][:, :], func=AF.Identity, scale=fcol(l, 2, cbk)),
                      reads=[("ps", bk), "fcw"], writes=[("tmpf", ia)])
                    eng, h = "dve", vector
                    for k in (1, 0):
                        E(eng, lambda fi=fi, ia=ia, cbk=cbk, k=k, base=base, h=h: h.scalar_tensor_tensor(
                            out=tmpf[ia][:], in0=fGU[fi][:, base + k:base + k + T], scalar=fcol(l, k, cbk),
                            in1=tmpf[ia][:], op0=ALU.mult, op1=ALU.add),
                          reads=[("fGh", fi), ("fGm", fi, hh), ("tmpf", ia), "fcw"], writes=[("tmpf", ia)])
                if False:
                    pass
                else:
                    E("act", lambda hofs=hofs, bgt=bgt: scalar.activation(
                        out=hF[:, hofs:hofs + 2], in_=ps[bgt][:, T - 2:T], func=AF.Copy),
                      reads=[("ps", bgt)], writes=["hF"])
                    E("act", lambda hofs=hofs, bup=bup: scalar.activation(
                        out=hF[:, hofs + 2:hofs + 4], in_=ps[bup][:, T - 2:T], func=AF.Copy),
                      reads=[("ps", bup)], writes=["hF"])
                ig, iu = accs
                E("act", lambda ig=ig: scalar.activation(out=tmpf[ig][:], in_=tmpf[ig][:], func=AF.Silu),
                  reads=[("tmpf", ig)], writes=[("tmpf", ig)])
                fe, fh = ("pool", gpsimd) if (os.environ.get("F_POOL", "1") == "1" and t > 0) else ("dve", vector)
                E(fe, lambda j=j, ig=ig, iu=iu, fh=fh: fh.tensor_tensor(out=zblk(j), in0=tmpf[iu][:],
                                                                         in1=tmpf[ig][:], op=ALU.mult),
                  reads=[("tmpf", ig), ("tmpf", iu)], writes=[("z", j)])
        cbase += 11
        zk = [("z", j) for j in range(NJ)]
        if t == dbg_t and l == 0:
            dbg("zf", bufA, NJ * T, zk)
        def wd_groups():
            for i in range(4):
                s, _ = stream.get(t, l, cbase + i)
                sk = slot_keys(s)
                for jj in range(2):
                    pairs = [(ring[s][:, kc * 256 + jj * P:kc * 256 + (jj + 1) * P], zblk(kc)) for kc in range(NJ)]
                    if i == 0 and jj == 0:
                        yield (pairs, sk, [[("z", kc)] for kc in range(NJ)])
                    else:
                        yield (pairs, sk + zk)
        resid_phase(wd_groups())
        cbase += 4
        if t == dbg_t and l == 0:
            dbg("x2", xT, KC * T, xkeys)
        return cbase

    def final_out(t):
        out_keys = []
        if do_final:
            rstd_finish(T)
            scale_rows(cb, "cb", 2 * NPRM)
            src, skey = cb, "cb"
            if t == dbg_t:
                dbg("yT", cb, KC * T, [("cb", i) for i in range(8)])
        else:
            src, skey = xT, "xT"
        for blk in range(4):
            yi = rot("yout", 2)
            for hb in range(2):
                b = nbank()

                def fn(blk=blk, hb=hb, b=b):
                    return [tensor.transpose(ps[b][:, i * P:(i + 1) * P],
                                             src[:, (4 * hb + i) * T + blk * P:(4 * hb + i) * T + (blk + 1) * P],
                                             ident[:]) for i in range(4)]
                E("pe", fn, reads=[(skey, 4 * hb + i) for i in range(4)] + ["ident"], writes=[("ps", b)])
                E("act", lambda yi=yi, hb=hb, b=b: scalar.activation(out=yout[yi][:, hb * 512:(hb + 1) * 512],
                                                                      in_=ps[b][:, :], func=AF.Copy),
                  reads=[("ps", b)], writes=[("yout", yi)])
            E("sp", lambda yi=yi, blk=blk: sync.dma_start(out=y_v[4 * t + blk], in_=yout[yi][:]),
              reads=[("yout", yi)], writes=[("y", 4 * t + blk)], dma=f"yout{yi}")
            out_keys.append(("y", 4 * t + blk))
        return out_keys

    dbg_keys = []

    def dbg(name, buf, n, keys):
        if not debug:
            return
        dt = buf.dtype if hasattr(buf, "dtype") else F32
        dd = nc.dram_tensor("dbg_" + name, [P, n], dt, kind="ExternalOutput").ap()
        E("sp", lambda: sync.dma_start(out=dd, in_=buf[:, :n]), reads=keys, writes=[("dbg", name)], dma="dbg")
        dbg_keys.append(("dbg", name))

    all_out = []
    for t in range(n_tiles):
        load_x_tile(t)
        for l in range(depth):
            cbase = mixer(t, l)
            cbase = ffn(t, l, cbase)
            assert cbase == len(chunks)
        all_out += final_out(t)
    trk.wait_all("sp", all_out + dbg_keys)
    global _LAST_TRK
    _LAST_TRK = trk
    return nc


_NC_CACHE = {}


def kernel(x, mem, norm_mix_g, norm_mem_g, w_in, b_gate, conv_a_w, w_a_out, conv_b_w, conv_b_bias,
           ln_b_g, ln_b_b, w_b_out, w_kv, w_att_out, w_o, norm_ffn_g, w_up, conv_ffn_w, w_down, norm_final_g):
    f = lambda a: np.ascontiguousarray(np.asarray(a, dtype=np.float32))
    x = f(x)
    mem = f(mem)
    shared = dict(norm_mix_g=f(norm_mix_g), norm_mem_g=f(norm_mem_g), w_in=f(w_in), b_gate=f(b_gate),
                  conv_a_w=f(conv_a_w), w_a_out=f(w_a_out), conv_b_w=f(conv_b_w), conv_b_bias=f(conv_b_bias),
                  ln_b_g=f(ln_b_g), ln_b_b=f(ln_b_b), w_b_out=f(w_b_out), w_kv=f(w_kv), w_att_out=f(w_att_out),
                  w_o=f(w_o), norm_ffn_g=f(norm_ffn_g), w_up=f(w_up), conv_ffn_w=f(conv_ffn_w), w_down=f(w_down),
                  norm_final_g=f(norm_final_g))
    nb = x.shape[0]
    if "nc" not in _NC_CACHE:
        _NC_CACHE["nc"] = build_program()
    nc = _NC_CACHE["nc"]
    in_maps = [dict(shared, x=x[b], mem=mem[b]) for b in range(nb)]
    res = run_bass_kernel_spmd(nc, in_maps, core_ids=list(range(nb)))
    return np.stack([r["y"] for r in res.results], axis=0)
```

```python
import os
import numpy as np
import concourse.bass as bass
import concourse.mybir as mybir
from concourse.bass_utils import run_bass_kernel_spmd

F32 = mybir.dt.float32
BF16 = mybir.dt.bfloat16
AF = mybir.ActivationFunctionType
ALU = mybir.AluOpType

P = 128
D = 1024
KC = 8
T = 512
SEQ = 4096
NMEM = 256
DFF = 2816
NJ = 22
NL = 2
DIN = 9216
EPS = 1e-6
SLOT = 6144
NSLOT = 5
NROT = 6
NPRM = 43
NTD = int(os.environ.get("NTD", "8"))


class Trk:
    def __init__(self, nc):
        self.nc = nc
        self.eng = {}
        for name, h in (("pe", nc.tensor), ("act", nc.scalar), ("dve", nc.vector),
                        ("pool", nc.gpsimd), ("sp", nc.sync)):
            self.eng[name] = dict(h=h, sem=nc.alloc_semaphore("s_" + name), cnt=0, seen={})
        self.lastw = {}
        self.readers = {}
        self.dsem = {}

    def _dma_sem(self, name):
        if name not in self.dsem:
            self.dsem[name] = [self.nc.alloc_semaphore("d_" + name), 0]
        return self.dsem[name]

    def emit(self, eng, fn, reads=(), writes=(), dma=None):
        e = self.eng[eng]
        need = {}

        def add(t):
            sname, sem, val, owner = t
            if owner == eng:
                return
            if owner.startswith("dma:"):
                val = self.dsem[owner[4:]][1]
            if e["seen"].get(sname, 0) >= val:
                return
            if sname not in need or need[sname][1] < val:
                need[sname] = (sem, val)

        for k in reads:
            t = self.lastw.get(k)
            if t is not None:
                add(t)
        for k in writes:
            t = self.lastw.get(k)
            if t is not None:
                add(t)
            for t in self.readers.get(k, {}).values():
                add(t)
        waits = list(need.items())
        embed = not (eng == "pool" and dma is None)
        for sname, (sem, val) in (waits[:-1] if embed else waits):
            e["h"].wait_ge(sem, val)
            e["seen"][sname] = val
        insts = fn()
        if not isinstance(insts, (list, tuple)):
            insts = [insts]
        if waits and embed:
            sname, (sem, val) = waits[-1]
            insts[0]._wait_ge(sem, val)
            e["seen"][sname] = val
        if dma is not None:
            ds = self._dma_sem(dma)
            ds[1] += 16
            insts[-1].then_inc(ds[0], 16)
            tk = ("d_" + dma, ds[0], ds[1], "dma:" + dma)
        else:
            e["cnt"] += 1
            insts[-1].then_inc(e["sem"], 1)
            tk = ("s_" + eng, e["sem"], e["cnt"], eng)
        for k in writes:
            self.lastw[k] = tk
            self.readers[k] = {}
        for k in reads:
            if k in writes:
                continue
            r = self.readers.setdefault(k, {})
            old = r.get(tk[0])
            if old is None or old[2] < tk[2]:
                r[tk[0]] = tk
        return tk

    def barrier(self):
        for en, e in self.eng.items():
            for on, o in self.eng.items():
                if on == en or o["cnt"] == 0:
                    continue
                if e["seen"].get("s_" + on, 0) >= o["cnt"]:
                    continue
                e["h"].wait_ge(o["sem"], o["cnt"])
                e["seen"]["s_" + on] = o["cnt"]
            for dn, (sem, cnt) in self.dsem.items():
                if cnt == 0 or e["seen"].get("d_" + dn, 0) >= cnt:
                    continue
                e["h"].wait_ge(sem, cnt)
                e["seen"]["d_" + dn] = cnt

    def wait_all(self, eng, keys):
        e = self.eng[eng]
        for k in keys:
            t = self.lastw.get(k)
            if t is None:
                continue
            sname, sem, val, owner = t
            if e["seen"].get(sname, 0) >= val:
                continue
            e["h"].wait_ge(sem, val)
            e["seen"][sname] = val


def build_program(n_tiles=SEQ // T, depth=NL, do_final=True, debug=False, dbg_t=0):
    nc = bass.Bass("TRN2", target_bir_lowering=False)
    trk = Trk(nc)
    ntok = n_tiles * T

    def din(name, shape):
        return nc.dram_tensor(name, list(shape), F32, kind="ExternalInput").ap()

    x_d = din("x", (ntok, D))
    mem_d = din("mem", (NMEM, D))
    norm_mix_g = din("norm_mix_g", (NL, D))
    norm_mem_g = din("norm_mem_g", (NL, D))
    w_in = din("w_in", (NL, D, DIN))
    b_gate = din("b_gate", (NL, 3 * D))
    conv_a_w = din("conv_a_w", (NL, 3, D))
    w_a_out = din("w_a_out", (NL, D, D))
    conv_b_w = din("conv_b_w", (NL, 31, D))
    conv_b_bias = din("conv_b_bias", (NL, D))
    ln_b_g = din("ln_b_g", (NL, D))
    ln_b_b = din("ln_b_b", (NL, D))
    w_b_out = din("w_b_out", (NL, D, D))
    w_kv = din("w_kv", (NL, D, 2 * D))
    w_att_out = din("w_att_out", (NL, D, D))
    w_o = din("w_o", (NL, D, D))
    norm_ffn_g = din("norm_ffn_g", (NL, D))
    w_up = din("w_up", (NL, D, 2 * DFF))
    conv_ffn_w = din("conv_ffn_w", (NL, 3, 2 * DFF))
    w_down = din("w_down", (NL, DFF, D))
    norm_final_g = din("norm_final_g", (D,))
    y_d = nc.dram_tensor("y", [ntok, D], F32, kind="ExternalOutput").ap()

    def mk_chunks():
        ch = []
        for i in range(4):
            ch.append(("A", i, 6144))
        for c in range(8):
            ch.append(("B", c, 2 * 1024 + 31 * 128))
        for i in range(2):
            ch.append(("Q", i, 4096))
        for i in range(12):
            ch.append(("O", i, 4096))
        for i in range(2):
            ch.append(("WO", i, 4096))
        for jj in range(11):
            ch.append(("F", jj, 2 * 2048))
        for i in range(4):
            ch.append(("D", i, 2 * NJ * 128))
        return ch

    chunks = mk_chunks()
    offs = []
    o = 0
    for (_, _, sz) in chunks:
        assert sz <= SLOT
        offs.append(o)
        o += sz
    TOT = o
    wsc = [nc.dram_tensor(f"wsc{l}", [P, TOT], BF16).ap() for l in range(depth)]

    def sb(name, n, dt):
        return nc.alloc_sbuf_tensor(name, [P, n], dt)

    from contextlib import ExitStack
    ring = [sb(f"ring{i}", SLOT, BF16) for i in range(NSLOT)]
    kT = [sb(f"kT{l}", KC * NMEM, BF16) for l in range(depth)]
    vS = [sb(f"vS{l}", 2 * D, BF16) for l in range(depth)]
    ident = sb("ident", P, F32)
    identb = sb("identb", P, BF16)
    onesb = sb("onesb", P, BF16)
    epsc = sb("epsc", 1, F32)
    prm = sb("prm", KC * P, F32)
    fcw = sb("fcw", 44 * 8, F32)
    hA = sb("hA", NL * 8 * 2, BF16)
    hB = sb("hB", NL * 8 * 30, BF16)
    hF = sb("hF", NL * NJ * 2 * 2, BF16)
    tmpb = [sb(f"tmpb{i}", T, BF16) for i in range(4)]
    rstd = sb("rstd", T, F32)
    xin = [sb(f"xin{i}", D, F32) for i in range(2)]
    setup_stack = ExitStack()

    def sb_tmp(name, n, dt):
        return setup_stack.enter_context(nc.sbuf_tensor(name, [P, n], dt))

    prow = sb_tmp("prow", D, F32)
    frow = sb_tmp("frow", 2 * DFF, F32)
    memT = sb_tmp("memT", KC * NMEM, F32)
    memn = sb_tmp("memn", KC * NMEM, BF16)
    ps = [nc.alloc_psum_tensor(f"ps{i}", [P, T], F32) for i in range(8)]

    E = trk.emit
    tensor, scalar, vector, gpsimd, sync = nc.tensor, nc.scalar, nc.vector, nc.gpsimd, nc.sync

    state = dict(bank=0, tf=0, tb=0, slot=0, xin=0, yout=0, tA=0, gB=0, fGU=0)

    def nbank():
        b = state["bank"]
        state["bank"] = (b + 1) % NROT
        return b

    def ntf():
        i = state["tf"]
        state["tf"] = (i + 1) % len(tmpf)
        return i

    def ntb():
        i = state["tb"]
        state["tb"] = (i + 1) % len(tmpb)
        return i

    def rot(name, n):
        i = state[name]
        state[name] = (i + 1) % n
        return i

    def slot_keys(s):
        return [("slot", s, i) for i in range(8)]

    def mm(bank, pairs, reads, n=T, fine=None):
        if fine is not None:
            for i, (l_ap, r_ap) in enumerate(pairs):
                E("pe", lambda i=i, l_ap=l_ap, r_ap=r_ap: tensor.matmul(
                    ps[bank][:, :n], lhsT=l_ap, rhs=r_ap, start=(i == 0), stop=(i == len(pairs) - 1)),
                  reads=list(reads) + fine[i], writes=[("ps", bank)])
            return
        def fn():
            out = []
            for i, (l_ap, r_ap) in enumerate(pairs):
                out.append(tensor.matmul(ps[bank][:, :n], lhsT=l_ap, rhs=r_ap,
                                         start=(i == 0), stop=(i == len(pairs) - 1)))
            return out
        return E("pe", fn, reads=reads, writes=[("ps", bank)])

    def prmcol(r, c):
        return prm[:, c * P + r:c * P + r + 1]

    E("pool", lambda: gpsimd.memset(ident[:], 1.0), writes=["ident"])
    E("pool", lambda: gpsimd.affine_select(out=ident[:], in_=ident[:], pattern=[[-1, P]],
                                           compare_op=ALU.is_equal, fill=0.0, base=0, channel_multiplier=1),
      reads=["ident"], writes=["ident"])
    E("dve", lambda: vector.tensor_copy(out=identb[:], in_=ident[:]), reads=["ident"], writes=["identb"])
    E("dve", lambda: vector.memset(onesb[:], 1.0), writes=["onesb"])
    E("dve", lambda: vector.memset(epsc[:], EPS), writes=["epsc"])
    E("dve", lambda: vector.memset(hA[:], 0.0), writes=["hA"])
    E("dve", lambda: vector.memset(hB[:], 0.0), writes=["hB"])
    E("dve", lambda: vector.memset(hF[:], 0.0), writes=["hF"])

    E("dve", lambda: vector.memset(prow[:], 0.0), writes=["prow_all"])

    def prow_dma(row0, nrows, src):
        E("sp", lambda: sync.dma_start(out=prow[row0:row0 + nrows, :], in_=src),
          reads=["prow_all"], writes=[("prow", row0)], dma="prm")

    prow_keys = []
    for l in range(NL):
        base = l * NPRM
        for (r, n, src) in ((0, 1, norm_mix_g[l:l + 1, :]), (1, 1, norm_mem_g[l:l + 1, :]),
                            (2, 3, b_gate[l].rearrange("(i n) -> i n", n=D)),
                            (5, 3, conv_a_w[l]), (8, 31, conv_b_w[l]),
                            (39, 1, conv_b_bias[l:l + 1, :]), (40, 1, ln_b_g[l:l + 1, :]),
                            (41, 1, ln_b_b[l:l + 1, :]), (42, 1, norm_ffn_g[l:l + 1, :])):
            prow_dma(base + r, n, src)
            prow_keys.append(("prow", base + r))
    prow_dma(2 * NPRM, 1, norm_final_g.rearrange("(o n) -> o n", o=1))
    prow_keys.append(("prow", 2 * NPRM))
    NR = P
    E("sp", lambda: sync.dma_start(out=frow[0:6, :], in_=conv_ffn_w.rearrange("l k n -> (l k) n")),
      writes=["frow"], dma="prm")

    for g in range(2):
        b = nbank()

        def fn(g=g, b=b):
            return [tensor.transpose(ps[b][:, i * P:i * P + NR], prow[0:NR, (4 * g + i) * P:(4 * g + i + 1) * P],
                                     ident[0:NR, 0:NR]) for i in range(4)]
        E("pe", fn, reads=prow_keys + ["ident"], writes=[("ps", b)])
        E("dve", lambda g=g, b=b: vector.tensor_copy(
            out=prm[:, 4 * g * P:(4 * g + 4) * P].rearrange("p (a r) -> p a r", r=P)[:, :, 0:NR],
            in_=ps[b][:, :].rearrange("p (a r) -> p a r", r=P)[:, :, 0:NR]),
          reads=[("ps", b)], writes=["prm"])
    for g in range(11):
        b = nbank()

        def fn(g=g, b=b):
            return [tensor.transpose(ps[b][:, i * P:i * P + 6], frow[0:6, (4 * g + i) * P:(4 * g + i + 1) * P],
                                     ident[0:6, 0:6]) for i in range(4)]
        E("pe", fn, reads=["frow", "ident"], writes=[("ps", b)])
        E("dve", lambda g=g, b=b: vector.tensor_copy(
            out=fcw[:, g * 32:(g + 1) * 32].rearrange("p (a r) -> p a r", r=8)[:, :, 0:6],
            in_=ps[b][:, :].rearrange("p (a r) -> p a r", r=P)[:, :, 0:6]),
          reads=[("ps", b)], writes=["fcw"])

    def fcol(l, k, cbk):
        i = cbk * 8 + l * 3 + k
        return fcw[:, i:i + 1]

    def stat_sq(src, n, c, keys):
        ib = ntb()
        E("act", lambda: scalar.activation(out=tmpb[ib][:, :n], in_=src[:, c * n:(c + 1) * n], func=AF.Square),
          reads=keys, writes=[("tmpb", ib)])
        return (c, ib, n)

    def stat_mm(h):
        c, ib, n = h
        E("pe", lambda: tensor.matmul(ps[7][:, :n], lhsT=onesb[:], rhs=tmpb[ib][:, :n],
                                      start=(c == 0), stop=(c == KC - 1)),
          reads=[("tmpb", ib), "onesb"], writes=[("ps", 7)])

    def rstd_finish(n):
        E("act", lambda: scalar.activation(out=rstd[:, :n], in_=ps[7][:, :n], func=AF.Sqrt, scale=1.0 / D,
                                           bias=epsc[:, 0:1]),
          reads=[("ps", 7), "epsc"], writes=["rstd"])
        E("dve", lambda: vector.reciprocal(out=rstd[:, :n], in_=rstd[:, :n]), reads=["rstd"], writes=["rstd"])

    def rms_rstd(src, n, ncols_key_reads):
        for c in range(KC):
            stat_mm(stat_sq(src, n, c, ncols_key_reads(c)))
        rstd_finish(n)

    def fill_w(s, part, dst_off, W2d, col0, ncol, K, sem):
        kc = K // P
        src = W2d[:, col0:col0 + ncol].rearrange("(kc p) c -> p kc c", p=P)
        dst = ring[s][:, dst_off:dst_off + kc * ncol].rearrange("p (kc c) -> p kc c", kc=kc)
        E("pool", lambda: gpsimd.dma_start(out=dst, in_=src), writes=[("slot", s, part)], dma=sem)

    def fill_diag(s, dst_off, col_ap, eng):
        if eng == "dve":
            return vector.tensor_scalar_mul(out=ring[s][:, dst_off:dst_off + P], in0=identb[:], scalar1=col_ap)
        return scalar.activation(out=ring[s][:, dst_off:dst_off + P], in_=identb[:], func=AF.Identity, scale=col_ap)

    mem_v = mem_d.rearrange("(b p) d -> b p d", p=P)
    for blk in range(2):
        xi = rot("xin", 2)
        E("sp", lambda blk=blk, xi=xi: sync.dma_start(out=xin[xi][:], in_=mem_v[blk]),
          writes=[("xin", xi)], dma=f"xin{xi}")
        for hb in range(2):
            b = nbank()

            def fn(blk=blk, xi=xi, hb=hb, b=b):
                return [tensor.transpose(ps[b][:, i * P:(i + 1) * P],
                                         xin[xi][:, (4 * hb + i) * P:(4 * hb + i + 1) * P], ident[:])
                        for i in range(4)]
            E("pe", fn, reads=[("xin", xi), "ident"], writes=[("ps", b)])
            E("dve", lambda blk=blk, hb=hb, b=b: vector.tensor_copy(
                out=memT[:, 4 * hb * NMEM:(4 * hb + 4) * NMEM].rearrange("p (c m) -> p c m", m=NMEM)[:, :, blk * P:(blk + 1) * P],
                in_=ps[b][:, :].rearrange("p (c m) -> p c m", m=P)),
              reads=[("ps", b)], writes=[("memT", blk)])
    rms_rstd(memT, NMEM, lambda c: [("memT", 0), ("memT", 1)])
    for l in range(depth):
        for c in range(KC):
            E("dve", lambda l=l, c=c: vector.scalar_tensor_tensor(
                out=memn[:, c * NMEM:(c + 1) * NMEM], in0=memT[:, c * NMEM:(c + 1) * NMEM],
                scalar=prmcol(l * NPRM + 1, c), in1=rstd[:, :NMEM], op0=ALU.mult, op1=ALU.mult),
              reads=[("memT", 0), ("memT", 1), "prm", "rstd"], writes=["memn"])
        for i in range(2):
            s = rot("slot", NSLOT)
            for j in range(4):
                fill_w(s, j, j * 1024, w_kv[l], (4 * i + j) * P, P, D, f"slot{s}")
            for j in range(4):
                dc = 4 * i + j
                b = nbank()
                mm(b, [(ring[s][:, j * 1024 + kc * P:j * 1024 + (kc + 1) * P], memn[:, kc * NMEM:(kc + 1) * NMEM])
                       for kc in range(KC)], reads=slot_keys(s) + ["memn"], n=NMEM)
                E("act", lambda l=l, dc=dc, b=b: scalar.activation(out=kT[l][:, dc * NMEM:(dc + 1) * NMEM],
                                                                    in_=ps[b][:, :NMEM], func=AF.Copy),
                  reads=[("ps", b)], writes=[("kT", l)])
        for i in range(2):
            s = rot("slot", NSLOT)
            fill_w(s, 0, 0, w_kv[l], D + i * 512, 512, D, f"slot{s}")
            for mb in range(2):
                b = nbank()
                mm(b, [(memn[:, kc * NMEM + mb * P:kc * NMEM + (mb + 1) * P], ring[s][:, kc * 512:(kc + 1) * 512])
                       for kc in range(KC)], reads=slot_keys(s) + ["memn"])
                E("act", lambda l=l, mb=mb, i=i, b=b: scalar.activation(
                    out=vS[l][:, mb * D + i * 512:mb * D + (i + 1) * 512], in_=ps[b][:, :], func=AF.Copy),
                  reads=[("ps", b)], writes=[("vS", l)])

    cidx = {(k, i): n for n, (k, i, _) in enumerate(chunks)}
    dflip = [0]
    for l in range(depth):
        pb = l * NPRM
        for c in range(8):
            s = rot("slot", NSLOT)
            dflip[0] ^= 1
            eng = "dve" if dflip[0] else "act"
            E(eng, lambda s=s, c=c, eng=eng, pb=pb: [fill_diag(s, k * P, prmcol(pb + 8 + k, c), eng)
                                                      for k in range(31)],
              reads=["identb", "prm"], writes=slot_keys(s))
            cbi = cidx[("B", c)]
            E("sp", lambda s=s, l=l, cbi=cbi: sync.dma_start(
                out=wsc[l][:, offs[cbi] + 2048:offs[cbi] + 2048 + 3968], in_=ring[s][:, 0:3968]),
              reads=slot_keys(s), writes=[("wscd", l)], dma=f"wst{s}")

    def fill_diag_pool(s, dst_off, col_ap):
        return gpsimd.tensor_scalar_mul(out=ring[s][:, dst_off:dst_off + P], in0=identb[:], scalar1=col_ap)

    def fill_chunk(s, l, ci):
        pb = l * NPRM
        kind, idx, size = chunks[ci]
        wsize = size
        sem = f"slot{s}"
        if kind == "A":
            for j in range(3):
                fill_w(s, j, j * 2048, w_in[l], j * D + idx * 256, 256, D, sem)
        elif kind == "B":
            c = idx
            for j in range(2):
                fill_w(s, j, j * 1024, w_in[l], (3 + j) * D + c * P, P, D, sem)
            E("sp", lambda: sync.dma_start(out=ring[s][:, 2048:2048 + 3968],
                                           in_=wsc[l][:, offs[ci] + 2048:offs[ci] + 2048 + 3968]),
              reads=[("wscd", l)], writes=[("slot", s, 7)], dma=sem)
            wsize = 2048
        elif kind == "Q":
            fill_w(s, 0, 0, w_in[l], 5 * D + idx * 512, 512, D, sem)
        elif kind == "O":
            pair, bi = idx // 3, idx % 3
            wsrc = (w_a_out, w_b_out, w_att_out)[bi]
            fill_w(s, 0, 0, wsrc[l], pair * 256, 256, D, sem)
            fill_w(s, 1, 2048, w_in[l], (6 + bi) * D + pair * 256, 256, D, sem)
        elif kind == "WO":
            fill_w(s, 0, 0, w_o[l], idx * 512, 512, D, sem)
        elif kind == "F":
            fill_w(s, 0, 0, w_up[l], 2 * idx * P, 256, D, sem)
            fill_w(s, 1, 2048, w_up[l], DFF + 2 * idx * P, 256, D, sem)
        elif kind == "D":
            fill_w(s, 0, 0, w_down[l], 2 * idx * P, 256, DFF, sem)
        if n_tiles > 1:
            E("sp", lambda: sync.dma_start(out=wsc[l][:, offs[ci]:offs[ci] + wsize], in_=ring[s][:, :wsize]),
              reads=slot_keys(s), writes=[("wsc", l, ci)], dma=f"wst{s}")

    trk.barrier()
    setup_stack.close()
    xT = sb("xT", KC * T, F32)
    hT = sb("hT", KC * T, BF16)
    bufA = sb("bufA", 24 * T, BF16)
    qm = sb("qm", KC * T, BF16)
    cb = sb("cb", KC * T, F32)
    tA = [sb(f"tA{i}", 2 + T, BF16) for i in range(2)]
    gB = [sb(f"gB{i}", 30 + T, BF16) for i in range(2)]
    fGU = [sb(f"fGU{i}", 2 * (2 + T), BF16) for i in range(2)]
    tmpf = [sb(f"tmpf{i}", T, F32) for i in range(6)]
    pT = [sb(f"pT{i}", T, BF16) for i in range(4)]
    mean = sb("mean", T, F32)
    lnr = sb("lnr", T, F32)
    yout = [sb(f"yout{i}", D, F32) for i in range(2)]
    class Stream:
        def __init__(self):
            self.seq = [(t, l, ci) for t in range(n_tiles) for l in range(depth) for ci in range(len(chunks))]
            self.nload = 0
            self.ncons = 0
            self.slots = {}
            self.held = set()

        def load_next(self):
            if self.nload >= len(self.seq):
                return
            victim = self.nload - NSLOT
            assert victim < self.ncons and victim not in self.held, (victim, self.ncons, self.held)
            (t, l, ci) = self.seq[self.nload]
            s = rot("slot", NSLOT)
            size = chunks[ci][2]
            if t == 0:
                fill_chunk(s, l, ci)
            else:
                E("sp", lambda: sync.dma_start(out=ring[s][:, :size], in_=wsc[l][:, offs[ci]:offs[ci] + size]),
                  reads=[("wsc", l, ci)], writes=slot_keys(s), dma=f"slot{s}")
            self.slots[self.nload] = s
            self.nload += 1

        def get(self, t, l, ci, hold=False):
            assert self.seq[self.ncons] == (t, l, ci), (self.seq[self.ncons], (t, l, ci))
            while self.nload < min(self.ncons + NSLOT - 1, len(self.seq)):
                self.load_next()
            n = self.ncons
            s = self.slots.pop(n)
            if hold:
                self.held.add(n)
            self.ncons += 1
            return s, n

        def unhold(self, n):
            self.held.discard(n)

    stream = Stream()
    x_v = x_d.rearrange("(n p) d -> n p d", p=P)
    y_v = y_d.rearrange("(n p) d -> n p d", p=P)
    xkeys = [("xT", c) for c in range(KC)]
    hkeys = [("hT", c) for c in range(KC)]

    xpref = {}

    def x_dma(t, blk):
        xi = rot("xin", 2)
        E("sp", lambda: sync.dma_start(out=xin[xi][:], in_=x_v[4 * t + blk]),
          writes=[("xin", xi)], dma=f"xin{xi}")
        return xi

    def load_x_tile(t):
        for blk in range(4):
            xi = xpref.pop((t, blk)) if (t, blk) in xpref else x_dma(t, blk)
            for hb in range(2):
                b = nbank()

                def fn(xi=xi, hb=hb, b=b):
                    return [tensor.transpose(ps[b][:, i * P:(i + 1) * P],
                                             xin[xi][:, (4 * hb + i) * P:(4 * hb + i + 1) * P], ident[:])
                            for i in range(4)]
                E("pe", fn, reads=[("xin", xi), "ident"], writes=[("ps", b)])
                E("act", lambda blk=blk, hb=hb, b=b: scalar.activation(
                    out=xT[:, 4 * hb * T:(4 * hb + 4) * T].rearrange("p (c m) -> p c m", m=T)[:, :, blk * P:(blk + 1) * P],
                    in_=ps[b][:, :].rearrange("p (c m) -> p c m", m=P), func=AF.Copy),
                  reads=[("ps", b)], writes=[("xT", 4 * hb + i) for i in range(4)])
        if t + 1 < n_tiles:
            for blk in range(2):
                xpref[(t + 1, blk)] = x_dma(t + 1, blk)

    def scale_rows(dst, dkey, grow):
        for c in range(KC):
            eng, h = "dve", vector
            E(eng, lambda c=c, h=h: h.scalar_tensor_tensor(
                out=dst[:, c * T:(c + 1) * T], in0=xT[:, c * T:(c + 1) * T], scalar=prmcol(grow, c),
                in1=rstd[:, :], op0=ALU.mult, op1=ALU.mult),
              reads=[("xT", c), "rstd", "prm"], writes=[(dkey, c)])

    def norm_to_h(grow, have_stats):
        if have_stats:
            rstd_finish(T)
        else:
            rms_rstd(xT, T, lambda c: [("xT", c)])
        scale_rows(hT, "hT", grow)

    def resid_phase(groups):
        pend = None
        for dc, grp in enumerate(groups):
            pairs, reads = grp[0], grp[1]
            b = nbank()
            mm(b, pairs, reads, fine=(grp[2] if len(grp) > 2 else None))
            E("dve", lambda dc=dc, b=b: vector.tensor_tensor(out=xT[:, dc * T:(dc + 1) * T],
                                                              in0=xT[:, dc * T:(dc + 1) * T], in1=ps[b][:, :],
                                                              op=ALU.add),
              reads=[("ps", b), ("xT", dc)], writes=[("xT", dc)])
            h = stat_sq(xT, T, dc, [("xT", dc)])
            if pend is not None:
                stat_mm(pend)
            pend = h
        stat_mm(pend)

    def hrhs(kc):
        return hT[:, kc * T:(kc + 1) * T]

    def wblk(s, off, kc):
        return ring[s][:, off + kc * P:off + (kc + 1) * P]

    ZA, ZB, ZC = 0, 8, 16

    def zblk(i):
        return bufA[:, i * T:(i + 1) * T]

    def mixer(t, l):
        pb = l * NPRM
        D0 = (t == dbg_t and l == 0)
        if D0:
            dbg("xT", xT, KC * T, xkeys)
        norm_to_h(pb + 0, have_stats=(l > 0))
        if D0:
            dbg("hT", hT, KC * T, hkeys)
        cbase = 0
        def a_pair(pair):
            s, _ = stream.get(t, l, cbase + pair)
            sk = slot_keys(s)
            for cc in range(2):
                c = 2 * pair + cc
                bg, bc, bv = nbank(), nbank(), nbank()
                for j, b in enumerate((bg, bc, bv)):
                    pairs = [(ring[s][:, j * 2048 + kc * 256 + cc * P:j * 2048 + kc * 256 + (cc + 1) * P], hrhs(kc))
                             for kc in range(KC)]
                    if c == 0 and j == 0:
                        mm(b, pairs, reads=sk, fine=[[("hT", kc)] for kc in range(KC)])
                    else:
                        mm(b, pairs, reads=sk + hkeys)
                ig, ic = ntf(), ntf()
                E("act", lambda b=bc, i=ic: scalar.activation(out=tmpf[i][:], in_=ps[b][:, :], func=AF.Copy),
                  reads=[("ps", bc)], writes=[("tmpf", ic)])
                E("act", lambda b=bg, i=ig: scalar.activation(out=tmpf[i][:], in_=ps[b][:, :], func=AF.Copy),
                  reads=[("ps", bg)], writes=[("tmpf", ig)])
                ia = rot("tA", 2)
                hofs = (l * 8 + c) * 2
                E("dve", lambda ia=ia, hofs=hofs: vector.tensor_copy(out=tA[ia][:, 0:2], in_=hA[:, hofs:hofs + 2]),
                  reads=["hA"], writes=[("tA", ia)])
                E("dve", lambda ia=ia, ic=ic, bv=bv: vector.tensor_tensor(out=tA[ia][:, 2:2 + T], in0=ps[bv][:, :],
                                                                           in1=tmpf[ic][:], op=ALU.mult),
                  reads=[("ps", bv), ("tmpf", ic), ("tA", ia)], writes=[("tA", ia)])
                E("dve", lambda ic=ic, bv=bv, hofs=hofs: vector.tensor_tensor(
                    out=hA[:, hofs:hofs + 2], in0=ps[bv][:, T - 2:T], in1=tmpf[ic][:, T - 2:T], op=ALU.mult),
                  reads=[("ps", bv), ("tmpf", ic)], writes=["hA"])
                iacc = ntf()
                E("dve", lambda ia=ia, iacc=iacc, c=c: vector.tensor_scalar_mul(
                    out=tmpf[iacc][:], in0=tA[ia][:, 2:2 + T], scalar1=prmcol(pb + 5 + 2, c)),
                  reads=[("tA", ia), "prm"], writes=[("tmpf", iacc)])
                for k in (1, 0):
                    E("dve", lambda ia=ia, iacc=iacc, c=c, k=k: vector.scalar_tensor_tensor(
                        out=tmpf[iacc][:], in0=tA[ia][:, k:k + T], scalar=prmcol(pb + 5 + k, c),
                        in1=tmpf[iacc][:], op0=ALU.mult, op1=ALU.add),
                      reads=[("tA", ia), ("tmpf", iacc), "prm"], writes=[("tmpf", iacc)])
                E("dve", lambda c=c, iacc=iacc, ig=ig: vector.tensor_tensor(out=zblk(ZA + c), in0=tmpf[iacc][:],
                                                                            in1=tmpf[ig][:], op=ALU.mult),
                  reads=[("tmpf", iacc), ("tmpf", ig)], writes=[("z", ZA + c)])

        S1, S2 = 6, 7

        def b_part1(c):
            s, sn = stream.get(t, l, cbase + 4 + c, hold=True)
            sk = slot_keys(s)
            bu, bug = nbank(), nbank()
            mm(bu, [(wblk(s, 0, kc), hrhs(kc)) for kc in range(KC)], reads=sk + hkeys)
            mm(bug, [(wblk(s, 1024, kc), hrhs(kc)) for kc in range(KC)], reads=sk + hkeys)
            isg = ntf()
            E("act", lambda b=bug, i=isg: scalar.activation(out=tmpf[i][:], in_=ps[b][:, :], func=AF.Sigmoid),
              reads=[("ps", bug)], writes=[("tmpf", isg)])
            ig = rot("gB", 2)
            hofs = (l * 8 + c) * 30
            E("dve", lambda ig=ig, hofs=hofs: vector.tensor_copy(out=gB[ig][:, 0:30], in_=hB[:, hofs:hofs + 30]),
              reads=["hB"], writes=[("gB", ig)])
            E("dve", lambda ig=ig, isg=isg, bu=bu: vector.tensor_tensor(out=gB[ig][:, 30:30 + T], in0=ps[bu][:, :],
                                                                         in1=tmpf[isg][:], op=ALU.mult),
              reads=[("ps", bu), ("tmpf", isg), ("gB", ig)], writes=[("gB", ig)])
            E("dve", lambda isg=isg, bu=bu, hofs=hofs: vector.tensor_tensor(
                out=hB[:, hofs:hofs + 30], in0=ps[bu][:, T - 30:T], in1=tmpf[isg][:, T - 30:T], op=ALU.mult),
              reads=[("ps", bu), ("tmpf", isg)], writes=["hB"])
            return (c, s, sk, ig, sn)

        def b_part2(c, s, sk, ig, sn):
            bo = nbank()
            npe = 31 - NTD
            mm(bo, [(ring[s][:, 2048 + k * P:2048 + (k + 1) * P], gB[ig][:, k:k + T]) for k in range(npe)],
               reads=sk + [("gB", ig)])
            stream.unhold(sn)
            if NTD > 0:
                iacc = ntf()
                for k in range(npe, 31):
                    E("dve", lambda k=k, iacc=iacc, first=(k == npe): vector.scalar_tensor_tensor(
                        out=tmpf[iacc][:], in0=gB[ig][:, k:k + T], scalar=prmcol(pb + 8 + k, c),
                        in1=(ps[bo][:, :] if first else tmpf[iacc][:]), op0=ALU.mult, op1=ALU.add),
                      reads=[("gB", ig), "prm"] + ([("ps", bo)] if k == npe else [("tmpf", iacc)]),
                      writes=[("tmpf", iacc)])
                csrc, ckey = tmpf[iacc][:], ("tmpf", iacc)
            else:
                csrc, ckey = ps[bo][:, :], ("ps", bo)
            E("act", lambda c=c: scalar.activation(out=cb[:, c * T:(c + 1) * T], in_=csrc,
                                                   func=AF.Identity, bias=prmcol(pb + 39, c)),
              reads=[ckey, "prm"], writes=[("cb", c)])
            i1, i2 = ntb(), ntb()
            E("act", lambda c=c, i2=i2: scalar.activation(out=tmpb[i2][:], in_=csrc,
                                                          func=AF.Square, bias=prmcol(pb + 39, c)),
              reads=[ckey, "prm"], writes=[("tmpb", i2)])
            E("dve", lambda c=c, i1=i1: vector.tensor_copy(out=tmpb[i1][:], in_=cb[:, c * T:(c + 1) * T]),
              reads=[("cb", c)], writes=[("tmpb", i1)])
            return (c, i1, i2)

        def b_part3(c, i1, i2):
            E("pe", lambda c=c, i1=i1: tensor.matmul(ps[S1][:, :], lhsT=onesb[:], rhs=tmpb[i1][:],
                                                      start=(c == 0), stop=(c == 7)),
              reads=[("tmpb", i1), "onesb"], writes=[("ps", S1)])
            E("pe", lambda c=c, i2=i2: tensor.matmul(ps[S2][:, :], lhsT=onesb[:], rhs=tmpb[i2][:],
                                                      start=(c == 0), stop=(c == 7)),
              reads=[("tmpb", i2), "onesb"], writes=[("ps", S2)])

        for pair in range(4):
            a_pair(pair)
        pb1 = None
        pb2 = None
        for c in range(8):
            cur = b_part1(c)
            nxt2 = None
            if pb1 is not None:
                nxt2 = b_part2(*pb1)
            if pb2 is not None:
                b_part3(*pb2)
            pb1, pb2 = cur, nxt2
        last2 = b_part2(*pb1)
        cbase += 12

        def b_tail_stats():
            if pb2 is not None:
                b_part3(*pb2)
            b_part3(*last2)

        def ln_chain():
            E("dve", lambda: vector.tensor_scalar_mul(out=mean[:], in0=ps[S1][:, :], scalar1=1.0 / D),
              reads=[("ps", S1)], writes=["mean"])
            im = ntf()
            E("dve", lambda im=im: vector.tensor_tensor(out=tmpf[im][:], in0=mean[:], in1=mean[:], op=ALU.mult),
              reads=["mean"], writes=[("tmpf", im)])
            E("dve", lambda im=im: vector.scalar_tensor_tensor(out=lnr[:], in0=ps[S2][:, :], scalar=1.0 / D,
                                                                in1=tmpf[im][:], op0=ALU.mult, op1=ALU.subtract),
              reads=[("ps", S2), ("tmpf", im)], writes=["lnr"])
            E("act", lambda: scalar.activation(out=lnr[:], in_=lnr[:], func=AF.Sqrt, bias=epsc[:, 0:1]),
              reads=["lnr", "epsc"], writes=["lnr"])
            E("dve", lambda: vector.reciprocal(out=lnr[:], in_=lnr[:]), reads=["lnr"], writes=["lnr"])
            for c in range(8):
                E("pool", lambda c=c: gpsimd.tensor_tensor(out=cb[:, c * T:(c + 1) * T], in0=cb[:, c * T:(c + 1) * T],
                                                           in1=mean[:], op=ALU.subtract),
                  reads=[("cb", c), "mean"], writes=[("cb", c)])
                E("pool", lambda c=c: gpsimd.tensor_tensor(out=cb[:, c * T:(c + 1) * T], in0=cb[:, c * T:(c + 1) * T],
                                                           in1=lnr[:], op=ALU.mult),
                  reads=[("cb", c), "lnr"], writes=[("cb", c)])

        def ln_silu():
            for c in range(8):
                E("act", lambda c=c: scalar.activation(out=zblk(ZB + c), in_=cb[:, c * T:(c + 1) * T], func=AF.Silu,
                                                       bias=prmcol(pb + 41, c), scale=prmcol(pb + 40, c)),
                  reads=[("cb", c), "prm"], writes=[("z", ZB + c)])

        for i in range(2):
            s, _ = stream.get(t, l, cbase + i)
            sk = slot_keys(s)
            for j in range(4):
                dc = 4 * i + j
                b = nbank()
                mm(b, [(ring[s][:, kc * 512 + j * P:kc * 512 + (j + 1) * P], hrhs(kc)) for kc in range(KC)],
                   reads=sk + hkeys)
                E("act", lambda dc=dc, b=b: scalar.activation(out=qm[:, dc * T:(dc + 1) * T], in_=ps[b][:, :],
                                                               func=AF.Copy),
                  reads=[("ps", b)], writes=[("qm", dc)])
                if dc == 0:
                    b_tail_stats()
                if dc == 1:
                    ln_chain()
        cbase += 2
        def att_s(hd):
            for mb in range(2):
                b = nbank()
                ip = 2 * (hd % 2) + mb
                mm(b, [(kT[l][:, (2 * hd + kk) * NMEM + mb * P:(2 * hd + kk) * NMEM + (mb + 1) * P],
                        qm[:, (2 * hd + kk) * T:(2 * hd + kk + 1) * T]) for kk in range(2)],
                   reads=[("kT", l), ("qm", 2 * hd), ("qm", 2 * hd + 1)])
                E("act", lambda ip=ip, b=b: scalar.activation(out=pT[ip][:], in_=ps[b][:, :], func=AF.Exp,
                                                               scale=1.0 / 16.0),
                  reads=[("ps", b)], writes=[("pT", ip)])

        def att_o(hd):
            ips = [2 * (hd % 2), 2 * (hd % 2) + 1]
            pk = [("pT", ip) for ip in ips]
            bd = nbank()
            mm(bd, [(onesb[:], pT[ip][:]) for ip in ips], reads=pk + ["onesb"])
            ir = ntf()
            E("dve", lambda ir=ir, bd=bd: vector.reciprocal(out=tmpf[ir][:], in_=ps[bd][:, :]),
              reads=[("ps", bd)], writes=[("tmpf", ir)])
            for dd in range(2):
                dc = 2 * hd + dd
                b = nbank()
                mm(b, [(vS[l][:, mb * D + dc * P:mb * D + (dc + 1) * P], pT[ips[mb]][:]) for mb in range(2)],
                   reads=pk + [("vS", l)])
                E("dve", lambda dc=dc, b=b, ir=ir: vector.tensor_tensor(out=zblk(ZC + dc), in0=ps[b][:, :],
                                                                         in1=tmpf[ir][:], op=ALU.mult),
                  reads=[("ps", b), ("tmpf", ir)], writes=[("z", ZC + dc)])

        if os.environ.get("ATT_PIPE", "1") == "1":
            att_s(0)
            for hd in range(4):
                if hd + 1 < 4:
                    att_s(hd + 1)
                att_o(hd)
        else:
            for hd in range(4):
                att_s(hd)
                att_o(hd)
        ln_silu()
        if D0:
            dbg("zabc", bufA, 24 * T, [("z", i) for i in range(24)])
            dbg("qT", qm, KC * T, [("qm", i) for i in range(8)])
            dbg("cb", cb, KC * T, [("cb", i) for i in range(8)])
        zk = {ZA: [("z", ZA + i) for i in range(8)], ZB: [("z", ZB + i) for i in range(8)],
              ZC: [("z", ZC + i) for i in range(8)]}
        for pair in range(4):
            accs = [None, None]
            for bi, zb in enumerate((ZA, ZB, ZC)):
                s, _ = stream.get(t, l, cbase + pair * 3 + bi)
                sk = slot_keys(s)
                for cc in range(2):
                    c = 2 * pair + cc
                    by, bgt = nbank(), nbank()
                    mm(by, [(ring[s][:, kc * 256 + cc * P:kc * 256 + (cc + 1) * P], zblk(zb + kc))
                            for kc in range(KC)], reads=sk + zk[zb])
                    mm(bgt, [(ring[s][:, 2048 + kc * 256 + cc * P:2048 + kc * 256 + (cc + 1) * P], hrhs(kc))
                             for kc in range(KC)], reads=sk + hkeys)
                    igt = ntf()
                    E("act", lambda bgt=bgt, igt=igt, bi=bi, c=c: scalar.activation(
                        out=tmpf[igt][:], in_=ps[bgt][:, :], func=AF.Sigmoid, bias=prmcol(pb + 2 + bi, c)),
                      reads=[("ps", bgt), "prm"], writes=[("tmpf", igt)])
                    E("dve", lambda by=by, igt=igt: vector.tensor_tensor(out=tmpf[igt][:], in0=ps[by][:, :],
                                                                          in1=tmpf[igt][:], op=ALU.mult),
                      reads=[("ps", by), ("tmpf", igt)], writes=[("tmpf", igt)])
                    if bi == 0:
                        accs[cc] = igt
                    elif bi == 1:
                        acc = accs[cc]
                        E("dve", lambda acc=acc, igt=igt: vector.tensor_tensor(out=tmpf[acc][:], in0=tmpf[acc][:],
                                                                                in1=tmpf[igt][:], op=ALU.add),
                          reads=[("tmpf", acc), ("tmpf", igt)], writes=[("tmpf", acc)])
                    else:
                        acc = accs[cc]
                        E("dve", lambda acc=acc, igt=igt, c=c: vector.tensor_tensor(
                            out=qm[:, c * T:(c + 1) * T], in0=tmpf[acc][:], in1=tmpf[igt][:], op=ALU.add),
                          reads=[("tmpf", acc), ("tmpf", igt)], writes=[("qm", c)])
        cbase += 12
        qk = [("qm", i) for i in range(8)]
        if D0:
            dbg("mT", qm, KC * T, qk)
        def wo_groups():
            for i in range(2):
                s, _ = stream.get(t, l, cbase + i)
                sk = slot_keys(s)
                for j in range(4):
                    yield ([(ring[s][:, kc * 512 + j * P:kc * 512 + (j + 1) * P], qm[:, kc * T:(kc + 1) * T])
                            for kc in range(KC)], sk + qk)
        resid_phase(wo_groups())
        cbase += 2
        if D0:
            dbg("x1", xT, KC * T, xkeys)
        return cbase

    def ffn(t, l, cbase):
        pb = l * NPRM
        norm_to_h(pb + 42, have_stats=True)
        W2 = 2 + T
        for jj in range(11):
            s, _ = stream.get(t, l, cbase + jj)
            sk = slot_keys(s)
            for j2 in range(2):
                j = 2 * jj + j2
                o0 = j2 * 2048
                bgt, bup = nbank(), nbank()
                gtp = [(ring[s][:, kc * 256 + j2 * P:kc * 256 + (j2 + 1) * P], hrhs(kc)) for kc in range(KC)]
                upp = [(ring[s][:, 2048 + kc * 256 + j2 * P:2048 + kc * 256 + (j2 + 1) * P], hrhs(kc))
                       for kc in range(KC)]
                if j == 0:
                    mm(bgt, gtp, reads=sk, fine=[[("hT", kc)] for kc in range(KC)])
                else:
                    mm(bgt, gtp, reads=sk + hkeys)
                mm(bup, upp, reads=sk + hkeys)
                fi = rot("fGU", 2)
                hofs = (l * NJ + j) * 4
                fv = fGU[fi][:, :].rearrange("p (a w) -> p a w", w=W2)
                if False:
                    pass
                else:
                    E("act", lambda fi=fi, hofs=hofs, fv=fv: scalar.activation(
                        out=fv[:, :, 0:2], in_=hF[:, hofs:hofs + 4].rearrange("p (a w) -> p a w", w=2), func=AF.Copy),
                      reads=["hF"], writes=[("fGh", fi)])
                accs = []
                for hh, (bk, cbk) in enumerate(((bgt, j), (bup, NJ + j))):
                    ia = ntf()
                    accs.append(ia)
                    base = hh * W2
                    if hh == 0 and os.environ.get("F_DVECOPY", "0") == "1":
                        E("dve", lambda fi=fi, bk=bk, base=base: vector.tensor_copy(
                            out=fGU[fi][:, base + 2:base + 2 + T], in_=ps[bk][:, :]),
                          reads=[("ps", bk)], writes=[("fGm", fi, hh)])
                    else:
                        E("act", lambda fi=fi, bk=bk, base=base: scalar.activation(
                            out=fGU[fi][:, base + 2:base + 2 + T], in_=ps[bk][:, :], func=AF.Copy),
                          reads=[("ps", bk)], writes=[("fGm", fi, hh)])
                    E("act", lambda bk=bk, ia=ia, cbk=cbk: scalar.activation(
                        out=tmpf[ia][:], in_=ps[bk][:, :], func=AF.Identity, scale=fcol(l, 2, cbk)),
                      reads=[("ps", bk), "fcw"], writes=[("tmpf", ia)])
                    eng, h = "dve", vector
                    for k in (1, 0):
                        E(eng, lambda fi=fi, ia=ia, cbk=cbk, k=k, base=base, h=h: h.scalar_tensor_tensor(
                            out=tmpf[ia][:], in0=fGU[fi][:, base + k:base + k + T], scalar=fcol(l, k, cbk),
                            in1=tmpf[ia][:], op0=ALU.mult, op1=ALU.add),
                          reads=[("fGh", fi), ("fGm", fi, hh), ("tmpf", ia), "fcw"], writes=[("tmpf", ia)])
                if False:
                    pass
                else:
                    E("act", lambda hofs=hofs, bgt=bgt: scalar.activation(
                        out=hF[:, hofs:hofs + 2], in_=ps[bgt][:, T - 2:T], func=AF.Copy),
                      reads=[("ps", bgt)], writes=["hF"])
                    E("act", lambda hofs=hofs, bup=bup: scalar.activation(
                        out=hF[:, hofs + 2:hofs + 4], in_=ps[bup][:, T - 2:T], func=AF.Copy),
                      reads=[("ps", bup)], writes=["hF"])
                ig, iu = accs
                E("act", lambda ig=ig: scalar.activation(out=tmpf[ig][:], in_=tmpf[ig][:], func=AF.Silu),
                  reads=[("tmpf", ig)], writes=[("tmpf", ig)])
                fe, fh = ("pool", gpsimd) if (os.environ.get("F_POOL", "1") == "1" and t > 0) else ("dve", vector)
                E(fe, lambda j=j, ig=ig, iu=iu, fh=fh: fh.tensor_tensor(out=zblk(j), in0=tmpf[iu][:],
                                                                         in1=tmpf[ig][:], op=ALU.mult),
                  reads=[("tmpf", ig), ("tmpf", iu)], writes=[("z", j)])
        cbase += 11
        zk = [("z", j) for j in range(NJ)]
        if t == dbg_t and l == 0:
            dbg("zf", bufA, NJ * T, zk)
        NG, NE = 3, NJ - 2
        dslots = {}

        def dget(i):
            if i not in dslots:
                dslots[i] = stream.get(t, l, cbase + i, hold=True)
            return dslots[i]

        def wpairs(g):
            s, _ = dget(g // 2)
            jj = g % 2
            return s, [(ring[s][:, kc * 256 + jj * P:kc * 256 + (jj + 1) * P], zblk(kc)) for kc in range(NJ)]

        def resid_tail(dc, b, pend):
            E("dve", lambda: vector.tensor_tensor(out=xT[:, dc * T:(dc + 1) * T], in0=xT[:, dc * T:(dc + 1) * T],
                                                  in1=ps[b][:, :], op=ALU.add),
              reads=[("ps", b), ("xT", dc)], writes=[("xT", dc)])
            h = stat_sq(xT, T, dc, [("xT", dc)])
            if pend is not None:
                stat_mm(pend)
            return h

        obanks = {}
        for g in range(NG):
            s, prs = wpairs(g)
            b = nbank()
            obanks[g] = b
            for kc in range(NE):
                E("pe", lambda kc=kc, prs=prs, b=b: tensor.matmul(ps[b][:, :], lhsT=prs[kc][0], rhs=prs[kc][1],
                                                                  start=(kc == 0), stop=False),
                  reads=slot_keys(s) + [("z", kc)], writes=[("ps", b)])
        pend = None
        for g in range(8):
            s, prs = wpairs(g)
            if g < NG:
                b = obanks[g]
                for kc in range(NE, NJ):
                    E("pe", lambda kc=kc, prs=prs, b=b: tensor.matmul(ps[b][:, :], lhsT=prs[kc][0], rhs=prs[kc][1],
                                                                      start=False, stop=(kc == NJ - 1)),
                      reads=slot_keys(s) + [("z", kc)], writes=[("ps", b)])
            else:
                b = nbank()
                mm(b, prs, slot_keys(s) + zk)
            if g % 2 == 1:
                stream.unhold(dslots[g // 2][1])
            pend = resid_tail(g, b, pend)
        stat_mm(pend)
        cbase += 4
        if t == dbg_t and l == 0:
            dbg("x2", xT, KC * T, xkeys)
        return cbase

    def final_out(t):
        out_keys = []
        if do_final:
            rstd_finish(T)
            scale_rows(cb, "cb", 2 * NPRM)
            src, skey = cb, "cb"
            if t == dbg_t:
                dbg("yT", cb, KC * T, [("cb", i) for i in range(8)])
        else:
            src, skey = xT, "xT"
        for blk in range(4):
            yi = rot("yout", 2)
            for hb in range(2):
                b = nbank()

                def fn(blk=blk, hb=hb, b=b):
                    return [tensor.transpose(ps[b][:, i * P:(i + 1) * P],
                                             src[:, (4 * hb + i) * T + blk * P:(4 * hb + i) * T + (blk + 1) * P],
                                             ident[:]) for i in range(4)]
                E("pe", fn, reads=[(skey, 4 * hb + i) for i in range(4)] + ["ident"], writes=[("ps", b)])
                E("act", lambda yi=yi, hb=hb, b=b: scalar.activation(out=yout[yi][:, hb * 512:(hb + 1) * 512],
                                                                      in_=ps[b][:, :], func=AF.Copy),
                  reads=[("ps", b)], writes=[("yout", yi)])
            E("sp", lambda yi=yi, blk=blk: sync.dma_start(out=y_v[4 * t + blk], in_=yout[yi][:]),
              reads=[("yout", yi)], writes=[("y", 4 * t + blk)], dma=f"yout{yi}")
            out_keys.append(("y", 4 * t + blk))
        return out_keys

    dbg_keys = []

    def dbg(name, buf, n, keys):
        if not debug:
            return
        dt = buf.dtype if hasattr(buf, "dtype") else F32
        dd = nc.dram_tensor("dbg_" + name, [P, n], dt, kind="ExternalOutput").ap()
        E("sp", lambda: sync.dma_start(out=dd, in_=buf[:, :n]), reads=keys, writes=[("dbg", name)], dma="dbg")
        dbg_keys.append(("dbg", name))

    all_out = []
    for t in range(n_tiles):
        load_x_tile(t)
        for l in range(depth):
            cbase = mixer(t, l)
            cbase = ffn(t, l, cbase)
            assert cbase == len(chunks)
        all_out += final_out(t)
    trk.wait_all("sp", all_out + dbg_keys)
    global _LAST_TRK
    _LAST_TRK = trk
    return nc


_NC_CACHE = {}


def kernel(x, mem, norm_mix_g, norm_mem_g, w_in, b_gate, conv_a_w, w_a_out, conv_b_w, conv_b_bias,
           ln_b_g, ln_b_b, w_b_out, w_kv, w_att_out, w_o, norm_ffn_g, w_up, conv_ffn_w, w_down, norm_final_g):
    f = lambda a: np.ascontiguousarray(np.asarray(a, dtype=np.float32))
    x = f(x)
    mem = f(mem)
    shared = dict(norm_mix_g=f(norm_mix_g), norm_mem_g=f(norm_mem_g), w_in=f(w_in), b_gate=f(b_gate),
                  conv_a_w=f(conv_a_w), w_a_out=f(w_a_out), conv_b_w=f(conv_b_w), conv_b_bias=f(conv_b_bias),
                  ln_b_g=f(ln_b_g), ln_b_b=f(ln_b_b), w_b_out=f(w_b_out), w_kv=f(w_kv), w_att_out=f(w_att_out),
                  w_o=f(w_o), norm_ffn_g=f(norm_ffn_g), w_up=f(w_up), conv_ffn_w=f(conv_ffn_w), w_down=f(w_down),
                  norm_final_g=f(norm_final_g))
    nb = x.shape[0]
    if "nc" not in _NC_CACHE:
        _NC_CACHE["nc"] = build_program()
    nc = _NC_CACHE["nc"]
    in_maps = [dict(shared, x=x[b], mem=mem[b]) for b in range(nb)]
    res = run_bass_kernel_spmd(nc, in_maps, core_ids=list(range(nb)))
    return np.stack([r["y"] for r in res.results], axis=0)
```

```python
import os
import numpy as np
import concourse.bass as bass
import concourse.mybir as mybir
from concourse.bass_utils import run_bass_kernel_spmd

F32 = mybir.dt.float32
BF16 = mybir.dt.bfloat16
AF = mybir.ActivationFunctionType
ALU = mybir.AluOpType

P = 128
D = 1024
KC = 8
T = 512
SEQ = 4096
NMEM = 256
DFF = 2816
NJ = 22
NL = 2
DIN = 9216
EPS = 1e-6
SLOT = 6144
NSLOT = 5
NROT = 6
NPRM = 43
NTD = int(os.environ.get("NTD", "8"))


class Trk:
    def __init__(self, nc):
        self.nc = nc
        self.eng = {}
        for name, h in (("pe", nc.tensor), ("act", nc.scalar), ("dve", nc.vector),
                        ("pool", nc.gpsimd), ("sp", nc.sync)):
            self.eng[name] = dict(h=h, sem=nc.alloc_semaphore("s_" + name), cnt=0, seen={})
        self.lastw = {}
        self.readers = {}
        self.dsem = {}

    def _dma_sem(self, name):
        if name not in self.dsem:
            self.dsem[name] = [self.nc.alloc_semaphore("d_" + name), 0]
        return self.dsem[name]

    def emit(self, eng, fn, reads=(), writes=(), dma=None):
        e = self.eng[eng]
        need = {}

        def add(t):
            sname, sem, val, owner = t
            if owner == eng:
                return
            if owner.startswith("dma:"):
                val = self.dsem[owner[4:]][1]
            if e["seen"].get(sname, 0) >= val:
                return
            if sname not in need or need[sname][1] < val:
                need[sname] = (sem, val)

        for k in reads:
            t = self.lastw.get(k)
            if t is not None:
                add(t)
        for k in writes:
            t = self.lastw.get(k)
            if t is not None:
                add(t)
            for t in self.readers.get(k, {}).values():
                add(t)
        waits = list(need.items())
        embed = not (eng == "pool" and dma is None)
        for sname, (sem, val) in (waits[:-1] if embed else waits):
            e["h"].wait_ge(sem, val)
            e["seen"][sname] = val
        insts = fn()
        if not isinstance(insts, (list, tuple)):
            insts = [insts]
        if waits and embed:
            sname, (sem, val) = waits[-1]
            insts[0]._wait_ge(sem, val)
            e["seen"][sname] = val
        if dma is not None:
            ds = self._dma_sem(dma)
            ds[1] += 16
            insts[-1].then_inc(ds[0], 16)
            tk = ("d_" + dma, ds[0], ds[1], "dma:" + dma)
        else:
            e["cnt"] += 1
            insts[-1].then_inc(e["sem"], 1)
            tk = ("s_" + eng, e["sem"], e["cnt"], eng)
        for k in writes:
            self.lastw[k] = tk
            self.readers[k] = {}
        for k in reads:
            if k in writes:
                continue
            r = self.readers.setdefault(k, {})
            old = r.get(tk[0])
            if old is None or old[2] < tk[2]:
                r[tk[0]] = tk
        return tk

    def barrier(self):
        for en, e in self.eng.items():
            for on, o in self.eng.items():
                if on == en or o["cnt"] == 0:
                    continue
                if e["seen"].get("s_" + on, 0) >= o["cnt"]:
                    continue
                e["h"].wait_ge(o["sem"], o["cnt"])
                e["seen"]["s_" + on] = o["cnt"]
            for dn, (sem, cnt) in self.dsem.items():
                if cnt == 0 or e["seen"].get("d_" + dn, 0) >= cnt:
                    continue
                e["h"].wait_ge(sem, cnt)
                e["seen"]["d_" + dn] = cnt

    def wait_all(self, eng, keys):
        e = self.eng[eng]
        for k in keys:
            t = self.lastw.get(k)
            if t is None:
                continue
            sname, sem, val, owner = t
            if e["seen"].get(sname, 0) >= val:
                continue
            e["h"].wait_ge(sem, val)
            e["seen"][sname] = val


def build_program(n_tiles=SEQ // T, depth=NL, do_final=True, debug=False, dbg_t=0):
    nc = bass.Bass("TRN2", target_bir_lowering=False)
    trk = Trk(nc)
    ntok = n_tiles * T

    def din(name, shape):
        return nc.dram_tensor(name, list(shape), F32, kind="ExternalInput").ap()

    x_d = din("x", (ntok, D))
    mem_d = din("mem", (NMEM, D))
    norm_mix_g = din("norm_mix_g", (NL, D))
    norm_mem_g = din("norm_mem_g", (NL, D))
    w_in = din("w_in", (NL, D, DIN))
    b_gate = din("b_gate", (NL, 3 * D))
    conv_a_w = din("conv_a_w", (NL, 3, D))
    w_a_out = din("w_a_out", (NL, D, D))
    conv_b_w = din("conv_b_w", (NL, 31, D))
    conv_b_bias = din("conv_b_bias", (NL, D))
    ln_b_g = din("ln_b_g", (NL, D))
    ln_b_b = din("ln_b_b", (NL, D))
    w_b_out = din("w_b_out", (NL, D, D))
    w_kv = din("w_kv", (NL, D, 2 * D))
    w_att_out = din("w_att_out", (NL, D, D))
    w_o = din("w_o", (NL, D, D))
    norm_ffn_g = din("norm_ffn_g", (NL, D))
    w_up = din("w_up", (NL, D, 2 * DFF))
    conv_ffn_w = din("conv_ffn_w", (NL, 3, 2 * DFF))
    w_down = din("w_down", (NL, DFF, D))
    norm_final_g = din("norm_final_g", (D,))
    y_d = nc.dram_tensor("y", [ntok, D], F32, kind="ExternalOutput").ap()

    def mk_chunks():
        ch = []
        for i in range(4):
            ch.append(("A", i, 6144))
        for c in range(8):
            ch.append(("B", c, 2 * 1024 + 31 * 128))
        for i in range(2):
            ch.append(("Q", i, 4096))
        for i in range(12):
            ch.append(("O", i, 4096))
        for i in range(2):
            ch.append(("WO", i, 4096))
        for jj in range(11):
            ch.append(("F", jj, 2 * 2048))
        for i in range(4):
            ch.append(("D", i, 2 * NJ * 128))
        return ch

    chunks = mk_chunks()
    offs = []
    o = 0
    for (_, _, sz) in chunks:
        assert sz <= SLOT
        offs.append(o)
        o += sz
    TOT = o
    wsc = [nc.dram_tensor(f"wsc{l}", [P, TOT], BF16).ap() for l in range(depth)]

    def sb(name, n, dt):
        return nc.alloc_sbuf_tensor(name, [P, n], dt)

    from contextlib import ExitStack
    ring = [sb(f"ring{i}", SLOT, BF16) for i in range(NSLOT)]
    kT = [sb(f"kT{l}", KC * NMEM, BF16) for l in range(depth)]
    vS = [sb(f"vS{l}", 2 * D, BF16) for l in range(depth)]
    ident = sb("ident", P, F32)
    identb = sb("identb", P, BF16)
    onesb = sb("onesb", P, BF16)
    epsc = sb("epsc", 1, F32)
    prm = sb("prm", KC * P, F32)
    fcw = sb("fcw", 44 * 8, F32)
    hA = sb("hA", NL * 8 * 2, BF16)
    hB = sb("hB", NL * 8 * 30, BF16)
    hF = sb("hF", NL * NJ * 2 * 2, BF16)
    tmpb = [sb(f"tmpb{i}", T, BF16) for i in range(4)]
    rstd = sb("rstd", T, F32)
    xin = [sb(f"xin{i}", D, F32) for i in range(2)]
    setup_stack = ExitStack()

    def sb_tmp(name, n, dt):
        return setup_stack.enter_context(nc.sbuf_tensor(name, [P, n], dt))

    prow = sb_tmp("prow", D, F32)
    frow = sb_tmp("frow", 2 * DFF, F32)
    memT = sb_tmp("memT", KC * NMEM, F32)
    memn = sb_tmp("memn", KC * NMEM, BF16)
    ps = [nc.alloc_psum_tensor(f"ps{i}", [P, T], F32) for i in range(8)]

    E = trk.emit
    tensor, scalar, vector, gpsimd, sync = nc.tensor, nc.scalar, nc.vector, nc.gpsimd, nc.sync

    state = dict(bank=0, tf=0, tb=0, slot=0, xin=0, yout=0, tA=0, gB=0, fGU=0)

    def nbank():
        b = state["bank"]
        state["bank"] = (b + 1) % NROT
        return b

    def ntf():
        i = state["tf"]
        state["tf"] = (i + 1) % len(tmpf)
        return i

    def ntb():
        i = state["tb"]
        state["tb"] = (i + 1) % len(tmpb)
        return i

    def rot(name, n):
        i = state[name]
        state[name] = (i + 1) % n
        return i

    def slot_keys(s):
        return [("slot", s, i) for i in range(8)]

    def mm(bank, pairs, reads, n=T, fine=None):
        if fine is not None:
            for i, (l_ap, r_ap) in enumerate(pairs):
                E("pe", lambda i=i, l_ap=l_ap, r_ap=r_ap: tensor.matmul(
                    ps[bank][:, :n], lhsT=l_ap, rhs=r_ap, start=(i == 0), stop=(i == len(pairs) - 1)),
                  reads=list(reads) + fine[i], writes=[("ps", bank)])
            return
        def fn():
            out = []
            for i, (l_ap, r_ap) in enumerate(pairs):
                out.append(tensor.matmul(ps[bank][:, :n], lhsT=l_ap, rhs=r_ap,
                                         start=(i == 0), stop=(i == len(pairs) - 1)))
            return out
        return E("pe", fn, reads=reads, writes=[("ps", bank)])

    def prmcol(r, c):
        return prm[:, c * P + r:c * P + r + 1]

    E("pool", lambda: gpsimd.memset(ident[:], 1.0), writes=["ident"])
    E("pool", lambda: gpsimd.affine_select(out=ident[:], in_=ident[:], pattern=[[-1, P]],
                                           compare_op=ALU.is_equal, fill=0.0, base=0, channel_multiplier=1),
      reads=["ident"], writes=["ident"])
    E("dve", lambda: vector.tensor_copy(out=identb[:], in_=ident[:]), reads=["ident"], writes=["identb"])
    E("dve", lambda: vector.memset(onesb[:], 1.0), writes=["onesb"])
    E("dve", lambda: vector.memset(epsc[:], EPS), writes=["epsc"])
    E("dve", lambda: vector.memset(hA[:], 0.0), writes=["hA"])
    E("dve", lambda: vector.memset(hB[:], 0.0), writes=["hB"])
    E("dve", lambda: vector.memset(hF[:], 0.0), writes=["hF"])

    E("dve", lambda: vector.memset(prow[:], 0.0), writes=["prow_all"])

    def prow_dma(row0, nrows, src):
        E("sp", lambda: sync.dma_start(out=prow[row0:row0 + nrows, :], in_=src),
          reads=["prow_all"], writes=[("prow", row0)], dma="prm")

    prow_keys = []
    for l in range(NL):
        base = l * NPRM
        for (r, n, src) in ((0, 1, norm_mix_g[l:l + 1, :]), (1, 1, norm_mem_g[l:l + 1, :]),
                            (2, 3, b_gate[l].rearrange("(i n) -> i n", n=D)),
                            (5, 3, conv_a_w[l]), (8, 31, conv_b_w[l]),
                            (39, 1, conv_b_bias[l:l + 1, :]), (40, 1, ln_b_g[l:l + 1, :]),
                            (41, 1, ln_b_b[l:l + 1, :]), (42, 1, norm_ffn_g[l:l + 1, :])):
            prow_dma(base + r, n, src)
            prow_keys.append(("prow", base + r))
    prow_dma(2 * NPRM, 1, norm_final_g.rearrange("(o n) -> o n", o=1))
    prow_keys.append(("prow", 2 * NPRM))
    NR = P
    E("sp", lambda: sync.dma_start(out=frow[0:6, :], in_=conv_ffn_w.rearrange("l k n -> (l k) n")),
      writes=["frow"], dma="prm")

    for g in range(2):
        b = nbank()

        def fn(g=g, b=b):
            return [tensor.transpose(ps[b][:, i * P:i * P + NR], prow[0:NR, (4 * g + i) * P:(4 * g + i + 1) * P],
                                     ident[0:NR, 0:NR]) for i in range(4)]
        E("pe", fn, reads=prow_keys + ["ident"], writes=[("ps", b)])
        E("dve", lambda g=g, b=b: vector.tensor_copy(
            out=prm[:, 4 * g * P:(4 * g + 4) * P].rearrange("p (a r) -> p a r", r=P)[:, :, 0:NR],
            in_=ps[b][:, :].rearrange("p (a r) -> p a r", r=P)[:, :, 0:NR]),
          reads=[("ps", b)], writes=["prm"])
    for g in range(11):
        b = nbank()

        def fn(g=g, b=b):
            return [tensor.transpose(ps[b][:, i * P:i * P + 6], frow[0:6, (4 * g + i) * P:(4 * g + i + 1) * P],
                                     ident[0:6, 0:6]) for i in range(4)]
        E("pe", fn, reads=["frow", "ident"], writes=[("ps", b)])
        E("dve", lambda g=g, b=b: vector.tensor_copy(
            out=fcw[:, g * 32:(g + 1) * 32].rearrange("p (a r) -> p a r", r=8)[:, :, 0:6],
            in_=ps[b][:, :].rearrange("p (a r) -> p a r", r=P)[:, :, 0:6]),
          reads=[("ps", b)], writes=["fcw"])

    def fcol(l, k, cbk):
        i = cbk * 8 + l * 3 + k
        return fcw[:, i:i + 1]

    def stat_sq(src, n, c, keys):
        ib = ntb()
        E("act", lambda: scalar.activation(out=tmpb[ib][:, :n], in_=src[:, c * n:(c + 1) * n], func=AF.Square),
          reads=keys, writes=[("tmpb", ib)])
        return (c, ib, n)

    def stat_mm(h):
        c, ib, n = h
        E("pe", lambda: tensor.matmul(ps[7][:, :n], lhsT=onesb[:], rhs=tmpb[ib][:, :n],
                                      start=(c == 0), stop=(c == KC - 1)),
          reads=[("tmpb", ib), "onesb"], writes=[("ps", 7)])

    def rstd_finish(n):
        E("act", lambda: scalar.activation(out=rstd[:, :n], in_=ps[7][:, :n], func=AF.Sqrt, scale=1.0 / D,
                                           bias=epsc[:, 0:1]),
          reads=[("ps", 7), "epsc"], writes=["rstd"])
        E("dve", lambda: vector.reciprocal(out=rstd[:, :n], in_=rstd[:, :n]), reads=["rstd"], writes=["rstd"])

    def rms_rstd(src, n, ncols_key_reads):
        for c in range(KC):
            stat_mm(stat_sq(src, n, c, ncols_key_reads(c)))
        rstd_finish(n)

    def fill_w(s, part, dst_off, W2d, col0, ncol, K, sem):
        kc = K // P
        src = W2d[:, col0:col0 + ncol].rearrange("(kc p) c -> p kc c", p=P)
        dst = ring[s][:, dst_off:dst_off + kc * ncol].rearrange("p (kc c) -> p kc c", kc=kc)
        E("pool", lambda: gpsimd.dma_start(out=dst, in_=src), writes=[("slot", s, part)], dma=sem)

    def fill_diag(s, dst_off, col_ap, eng):
        if eng == "dve":
            return vector.tensor_scalar_mul(out=ring[s][:, dst_off:dst_off + P], in0=identb[:], scalar1=col_ap)
        return scalar.activation(out=ring[s][:, dst_off:dst_off + P], in_=identb[:], func=AF.Identity, scale=col_ap)

    mem_v = mem_d.rearrange("(b p) d -> b p d", p=P)
    for blk in range(2):
        xi = rot("xin", 2)
        E("sp", lambda blk=blk, xi=xi: sync.dma_start(out=xin[xi][:], in_=mem_v[blk]),
          writes=[("xin", xi)], dma=f"xin{xi}")
        for hb in range(2):
            b = nbank()

            def fn(blk=blk, xi=xi, hb=hb, b=b):
                return [tensor.transpose(ps[b][:, i * P:(i + 1) * P],
                                         xin[xi][:, (4 * hb + i) * P:(4 * hb + i + 1) * P], ident[:])
                        for i in range(4)]
            E("pe", fn, reads=[("xin", xi), "ident"], writes=[("ps", b)])
            E("dve", lambda blk=blk, hb=hb, b=b: vector.tensor_copy(
                out=memT[:, 4 * hb * NMEM:(4 * hb + 4) * NMEM].rearrange("p (c m) -> p c m", m=NMEM)[:, :, blk * P:(blk + 1) * P],
                in_=ps[b][:, :].rearrange("p (c m) -> p c m", m=P)),
              reads=[("ps", b)], writes=[("memT", blk)])
    rms_rstd(memT, NMEM, lambda c: [("memT", 0), ("memT", 1)])
    for l in range(depth):
        for c in range(KC):
            E("dve", lambda l=l, c=c: vector.scalar_tensor_tensor(
                out=memn[:, c * NMEM:(c + 1) * NMEM], in0=memT[:, c * NMEM:(c + 1) * NMEM],
                scalar=prmcol(l * NPRM + 1, c), in1=rstd[:, :NMEM], op0=ALU.mult, op1=ALU.mult),
              reads=[("memT", 0), ("memT", 1), "prm", "rstd"], writes=["memn"])
        for i in range(2):
            s = rot("slot", NSLOT)
            for j in range(4):
                fill_w(s, j, j * 1024, w_kv[l], (4 * i + j) * P, P, D, f"slot{s}")
            for j in range(4):
                dc = 4 * i + j
                b = nbank()
                mm(b, [(ring[s][:, j * 1024 + kc * P:j * 1024 + (kc + 1) * P], memn[:, kc * NMEM:(kc + 1) * NMEM])
                       for kc in range(KC)], reads=slot_keys(s) + ["memn"], n=NMEM)
                E("act", lambda l=l, dc=dc, b=b: scalar.activation(out=kT[l][:, dc * NMEM:(dc + 1) * NMEM],
                                                                    in_=ps[b][:, :NMEM], func=AF.Copy),
                  reads=[("ps", b)], writes=[("kT", l)])
        for i in range(2):
            s = rot("slot", NSLOT)
            fill_w(s, 0, 0, w_kv[l], D + i * 512, 512, D, f"slot{s}")
            for mb in range(2):
                b = nbank()
                mm(b, [(memn[:, kc * NMEM + mb * P:kc * NMEM + (mb + 1) * P], ring[s][:, kc * 512:(kc + 1) * 512])
                       for kc in range(KC)], reads=slot_keys(s) + ["memn"])
                E("act", lambda l=l, mb=mb, i=i, b=b: scalar.activation(
                    out=vS[l][:, mb * D + i * 512:mb * D + (i + 1) * 512], in_=ps[b][:, :], func=AF.Copy),
                  reads=[("ps", b)], writes=[("vS", l)])

    cidx = {(k, i): n for n, (k, i, _) in enumerate(chunks)}
    dflip = [0]
    for l in range(depth):
        pb = l * NPRM
        for c in range(8):
            s = rot("slot", NSLOT)
            dflip[0] ^= 1
            eng = "dve" if dflip[0] else "act"
            E(eng, lambda s=s, c=c, eng=eng, pb=pb: [fill_diag(s, k * P, prmcol(pb + 8 + k, c), eng)
                                                      for k in range(31)],
              reads=["identb", "prm"], writes=slot_keys(s))
            cbi = cidx[("B", c)]
            E("sp", lambda s=s, l=l, cbi=cbi: sync.dma_start(
                out=wsc[l][:, offs[cbi] + 2048:offs[cbi] + 2048 + 3968], in_=ring[s][:, 0:3968]),
              reads=slot_keys(s), writes=[("wscd", l)], dma=f"wst{s}")

    def fill_diag_pool(s, dst_off, col_ap):
        return gpsimd.tensor_scalar_mul(out=ring[s][:, dst_off:dst_off + P], in0=identb[:], scalar1=col_ap)

    def fill_chunk(s, l, ci):
        pb = l * NPRM
        kind, idx, size = chunks[ci]
        wsize = size
        sem = f"slot{s}"
        if kind == "A":
            for j in range(3):
                fill_w(s, j, j * 2048, w_in[l], j * D + idx * 256, 256, D, sem)
        elif kind == "B":
            c = idx
            for j in range(2):
                fill_w(s, j, j * 1024, w_in[l], (3 + j) * D + c * P, P, D, sem)
            E("sp", lambda: sync.dma_start(out=ring[s][:, 2048:2048 + 3968],
                                           in_=wsc[l][:, offs[ci] + 2048:offs[ci] + 2048 + 3968]),
              reads=[("wscd", l)], writes=[("slot", s, 7)], dma=sem)
            wsize = 2048
        elif kind == "Q":
            fill_w(s, 0, 0, w_in[l], 5 * D + idx * 512, 512, D, sem)
        elif kind == "O":
            pair, bi = idx // 3, idx % 3
            wsrc = (w_a_out, w_b_out, w_att_out)[bi]
            fill_w(s, 0, 0, wsrc[l], pair * 256, 256, D, sem)
            fill_w(s, 1, 2048, w_in[l], (6 + bi) * D + pair * 256, 256, D, sem)
        elif kind == "WO":
            fill_w(s, 0, 0, w_o[l], idx * 512, 512, D, sem)
        elif kind == "F":
            fill_w(s, 0, 0, w_up[l], 2 * idx * P, 256, D, sem)
            fill_w(s, 1, 2048, w_up[l], DFF + 2 * idx * P, 256, D, sem)
        elif kind == "D":
            fill_w(s, 0, 0, w_down[l], 2 * idx * P, 256, DFF, sem)
        if n_tiles > 1:
            E("sp", lambda: sync.dma_start(out=wsc[l][:, offs[ci]:offs[ci] + wsize], in_=ring[s][:, :wsize]),
              reads=slot_keys(s), writes=[("wsc", l, ci)], dma=f"wst{s}")

    trk.barrier()
    setup_stack.close()
    xT = sb("xT", KC * T, F32)
    hT = sb("hT", KC * T, BF16)
    bufA = sb("bufA", 24 * T, BF16)
    qm = sb("qm", KC * T, BF16)
    cb = sb("cb", KC * T, F32)
    tA = [sb(f"tA{i}", 2 + T, BF16) for i in range(2)]
    gB = [sb(f"gB{i}", 30 + T, BF16) for i in range(2)]
    fGU = [sb(f"fGU{i}", 2 * (2 + T), BF16) for i in range(2)]
    tmpf = [sb(f"tmpf{i}", T, F32) for i in range(6)]
    pT = [sb(f"pT{i}", T, BF16) for i in range(4)]
    mean = sb("mean", T, F32)
    lnr = sb("lnr", T, F32)
    yout = [sb(f"yout{i}", D, F32) for i in range(2)]
    class Stream:
        def __init__(self):
            self.seq = [(t, l, ci) for t in range(n_tiles) for l in range(depth) for ci in range(len(chunks))]
            self.nload = 0
            self.ncons = 0
            self.slots = {}
            self.held = set()

        def load_next(self):
            if self.nload >= len(self.seq):
                return
            victim = self.nload - NSLOT
            assert victim < self.ncons and victim not in self.held, (victim, self.ncons, self.held)
            (t, l, ci) = self.seq[self.nload]
            s = rot("slot", NSLOT)
            size = chunks[ci][2]
            if t == 0:
                fill_chunk(s, l, ci)
            else:
                E("sp", lambda: sync.dma_start(out=ring[s][:, :size], in_=wsc[l][:, offs[ci]:offs[ci] + size]),
                  reads=[("wsc", l, ci)], writes=slot_keys(s), dma=f"slot{s}")
            self.slots[self.nload] = s
            self.nload += 1

        def get(self, t, l, ci, hold=False):
            assert self.seq[self.ncons] == (t, l, ci), (self.seq[self.ncons], (t, l, ci))
            while self.nload < min(self.ncons + NSLOT - 1, len(self.seq)):
                self.load_next()
            n = self.ncons
            s = self.slots.pop(n)
            if hold:
                self.held.add(n)
            self.ncons += 1
            return s, n

        def unhold(self, n):
            self.held.discard(n)

    stream = Stream()
    x_v = x_d.rearrange("(n p) d -> n p d", p=P)
    y_v = y_d.rearrange("(n p) d -> n p d", p=P)
    xkeys = [("xT", c) for c in range(KC)]
    hkeys = [("hT", c) for c in range(KC)]

    xpref = {}

    def x_dma(t, blk):
        xi = rot("xin", 2)
        E("sp", lambda: sync.dma_start(out=xin[xi][:], in_=x_v[4 * t + blk]),
          writes=[("xin", xi)], dma=f"xin{xi}")
        return xi

    def load_x_tile(t):
        for blk in range(4):
            xi = xpref.pop((t, blk)) if (t, blk) in xpref else x_dma(t, blk)
            for hb in range(2):
                b = nbank()

                def fn(xi=xi, hb=hb, b=b):
                    return [tensor.transpose(ps[b][:, i * P:(i + 1) * P],
                                             xin[xi][:, (4 * hb + i) * P:(4 * hb + i + 1) * P], ident[:])
                            for i in range(4)]
                E("pe", fn, reads=[("xin", xi), "ident"], writes=[("ps", b)])
                E("act", lambda blk=blk, hb=hb, b=b: scalar.activation(
                    out=xT[:, 4 * hb * T:(4 * hb + 4) * T].rearrange("p (c m) -> p c m", m=T)[:, :, blk * P:(blk + 1) * P],
                    in_=ps[b][:, :].rearrange("p (c m) -> p c m", m=P), func=AF.Copy),
                  reads=[("ps", b)], writes=[("xT", 4 * hb + i) for i in range(4)])
        if t + 1 < n_tiles:
            for blk in range(2):
                xpref[(t + 1, blk)] = x_dma(t + 1, blk)

    def scale_rows(dst, dkey, grow):
        for c in range(KC):
            eng, h = "dve", vector
            E(eng, lambda c=c, h=h: h.scalar_tensor_tensor(
                out=dst[:, c * T:(c + 1) * T], in0=xT[:, c * T:(c + 1) * T], scalar=prmcol(grow, c),
                in1=rstd[:, :], op0=ALU.mult, op1=ALU.mult),
              reads=[("xT", c), "rstd", "prm"], writes=[(dkey, c)])

    def norm_to_h(grow, have_stats):
        if have_stats:
            rstd_finish(T)
        else:
            rms_rstd(xT, T, lambda c: [("xT", c)])
        scale_rows(hT, "hT", grow)

    def resid_phase(groups):
        pend = None
        for dc, grp in enumerate(groups):
            pairs, reads = grp[0], grp[1]
            b = nbank()
            mm(b, pairs, reads, fine=(grp[2] if len(grp) > 2 else None))
            E("dve", lambda dc=dc, b=b: vector.tensor_tensor(out=xT[:, dc * T:(dc + 1) * T],
                                                              in0=xT[:, dc * T:(dc + 1) * T], in1=ps[b][:, :],
                                                              op=ALU.add),
              reads=[("ps", b), ("xT", dc)], writes=[("xT", dc)])
            h = stat_sq(xT, T, dc, [("xT", dc)])
            if pend is not None:
                stat_mm(pend)
            pend = h
        stat_mm(pend)

    def hrhs(kc):
        return hT[:, kc * T:(kc + 1) * T]

    def wblk(s, off, kc):
        return ring[s][:, off + kc * P:off + (kc + 1) * P]

    ZA, ZB, ZC = 0, 8, 16

    def zblk(i):
        return bufA[:, i * T:(i + 1) * T]

    def mixer(t, l):
        pb = l * NPRM
        D0 = (t == dbg_t and l == 0)
        if D0:
            dbg("xT", xT, KC * T, xkeys)
        norm_to_h(pb + 0, have_stats=(l > 0))
        if D0:
            dbg("hT", hT, KC * T, hkeys)
        cbase = 0
        def a_pair(pair):
            s, _ = stream.get(t, l, cbase + pair)
            sk = slot_keys(s)
            for cc in range(2):
                c = 2 * pair + cc
                bg, bc, bv = nbank(), nbank(), nbank()
                for j, b in enumerate((bg, bc, bv)):
                    pairs = [(ring[s][:, j * 2048 + kc * 256 + cc * P:j * 2048 + kc * 256 + (cc + 1) * P], hrhs(kc))
                             for kc in range(KC)]
                    if c == 0 and j == 0:
                        mm(b, pairs, reads=sk, fine=[[("hT", kc)] for kc in range(KC)])
                    else:
                        mm(b, pairs, reads=sk + hkeys)
                ig, ic = ntf(), ntf()
                E("act", lambda b=bc, i=ic: scalar.activation(out=tmpf[i][:], in_=ps[b][:, :], func=AF.Copy),
                  reads=[("ps", bc)], writes=[("tmpf", ic)])
                E("act", lambda b=bg, i=ig: scalar.activation(out=tmpf[i][:], in_=ps[b][:, :], func=AF.Copy),
                  reads=[("ps", bg)], writes=[("tmpf", ig)])
                ia = rot("tA", 2)
                hofs = (l * 8 + c) * 2
                E("dve", lambda ia=ia, hofs=hofs: vector.tensor_copy(out=tA[ia][:, 0:2], in_=hA[:, hofs:hofs + 2]),
                  reads=["hA"], writes=[("tA", ia)])
                E("dve", lambda ia=ia, ic=ic, bv=bv: vector.tensor_tensor(out=tA[ia][:, 2:2 + T], in0=ps[bv][:, :],
                                                                           in1=tmpf[ic][:], op=ALU.mult),
                  reads=[("ps", bv), ("tmpf", ic), ("tA", ia)], writes=[("tA", ia)])
                E("dve", lambda ic=ic, bv=bv, hofs=hofs: vector.tensor_tensor(
                    out=hA[:, hofs:hofs + 2], in0=ps[bv][:, T - 2:T], in1=tmpf[ic][:, T - 2:T], op=ALU.mult),
                  reads=[("ps", bv), ("tmpf", ic)], writes=["hA"])
                iacc = ntf()
                E("dve", lambda ia=ia, iacc=iacc, c=c: vector.tensor_scalar_mul(
                    out=tmpf[iacc][:], in0=tA[ia][:, 2:2 + T], scalar1=prmcol(pb + 5 + 2, c)),
                  reads=[("tA", ia), "prm"], writes=[("tmpf", iacc)])
                for k in (1, 0):
                    E("dve", lambda ia=ia, iacc=iacc, c=c, k=k: vector.scalar_tensor_tensor(
                        out=tmpf[iacc][:], in0=tA[ia][:, k:k + T], scalar=prmcol(pb + 5 + k, c),
                        in1=tmpf[iacc][:], op0=ALU.mult, op1=ALU.add),
                      reads=[("tA", ia), ("tmpf", iacc), "prm"], writes=[("tmpf", iacc)])
                E("dve", lambda c=c, iacc=iacc, ig=ig: vector.tensor_tensor(out=zblk(ZA + c), in0=tmpf[iacc][:],
                                                                            in1=tmpf[ig][:], op=ALU.mult),
                  reads=[("tmpf", iacc), ("tmpf", ig)], writes=[("z", ZA + c)])

        S1, S2 = 6, 7

        def b_part1(c):
            s, sn = stream.get(t, l, cbase + 4 + c, hold=True)
            sk = slot_keys(s)
            bu, bug = nbank(), nbank()
            mm(bu, [(wblk(s, 0, kc), hrhs(kc)) for kc in range(KC)], reads=sk + hkeys)
            mm(bug, [(wblk(s, 1024, kc), hrhs(kc)) for kc in range(KC)], reads=sk + hkeys)
            isg = ntf()
            E("act", lambda b=bug, i=isg: scalar.activation(out=tmpf[i][:], in_=ps[b][:, :], func=AF.Sigmoid),
              reads=[("ps", bug)], writes=[("tmpf", isg)])
            ig = rot("gB", 2)
            hofs = (l * 8 + c) * 30
            E("dve", lambda ig=ig, hofs=hofs: vector.tensor_copy(out=gB[ig][:, 0:30], in_=hB[:, hofs:hofs + 30]),
              reads=["hB"], writes=[("gB", ig)])
            E("dve", lambda ig=ig, isg=isg, bu=bu: vector.tensor_tensor(out=gB[ig][:, 30:30 + T], in0=ps[bu][:, :],
                                                                         in1=tmpf[isg][:], op=ALU.mult),
              reads=[("ps", bu), ("tmpf", isg), ("gB", ig)], writes=[("gB", ig)])
            E("dve", lambda isg=isg, bu=bu, hofs=hofs: vector.tensor_tensor(
                out=hB[:, hofs:hofs + 30], in0=ps[bu][:, T - 30:T], in1=tmpf[isg][:, T - 30:T], op=ALU.mult),
              reads=[("ps", bu), ("tmpf", isg)], writes=["hB"])
            return (c, s, sk, ig, sn)

        def b_part2(c, s, sk, ig, sn):
            bo = nbank()
            npe = 31 - NTD
            mm(bo, [(ring[s][:, 2048 + k * P:2048 + (k + 1) * P], gB[ig][:, k:k + T]) for k in range(npe)],
               reads=sk + [("gB", ig)])
            stream.unhold(sn)
            if NTD > 0:
                iacc = ntf()
                for k in range(npe, 31):
                    E("dve", lambda k=k, iacc=iacc, first=(k == npe): vector.scalar_tensor_tensor(
                        out=tmpf[iacc][:], in0=gB[ig][:, k:k + T], scalar=prmcol(pb + 8 + k, c),
                        in1=(ps[bo][:, :] if first else tmpf[iacc][:]), op0=ALU.mult, op1=ALU.add),
                      reads=[("gB", ig), "prm"] + ([("ps", bo)] if k == npe else [("tmpf", iacc)]),
                      writes=[("tmpf", iacc)])
                csrc, ckey = tmpf[iacc][:], ("tmpf", iacc)
            else:
                csrc, ckey = ps[bo][:, :], ("ps", bo)
            E("act", lambda c=c: scalar.activation(out=cb[:, c * T:(c + 1) * T], in_=csrc,
                                                   func=AF.Identity, bias=prmcol(pb + 39, c)),
              reads=[ckey, "prm"], writes=[("cb", c)])
            i1, i2 = ntb(), ntb()
            E("act", lambda c=c, i2=i2: scalar.activation(out=tmpb[i2][:], in_=csrc,
                                                          func=AF.Square, bias=prmcol(pb + 39, c)),
              reads=[ckey, "prm"], writes=[("tmpb", i2)])
            E("dve", lambda c=c, i1=i1: vector.tensor_copy(out=tmpb[i1][:], in_=cb[:, c * T:(c + 1) * T]),
              reads=[("cb", c)], writes=[("tmpb", i1)])
            return (c, i1, i2)

        def b_part3(c, i1, i2):
            E("pe", lambda c=c, i1=i1: tensor.matmul(ps[S1][:, :], lhsT=onesb[:], rhs=tmpb[i1][:],
                                                      start=(c == 0), stop=(c == 7)),
              reads=[("tmpb", i1), "onesb"], writes=[("ps", S1)])
            E("pe", lambda c=c, i2=i2: tensor.matmul(ps[S2][:, :], lhsT=onesb[:], rhs=tmpb[i2][:],
                                                      start=(c == 0), stop=(c == 7)),
              reads=[("tmpb", i2), "onesb"], writes=[("ps", S2)])

        for pair in range(4):
            a_pair(pair)
        pb1 = None
        pb2 = None
        for c in range(8):
            cur = b_part1(c)
            nxt2 = None
            if pb1 is not None:
                nxt2 = b_part2(*pb1)
            if pb2 is not None:
                b_part3(*pb2)
            pb1, pb2 = cur, nxt2
        last2 = b_part2(*pb1)
        cbase += 12

        def b_tail_stats():
            if pb2 is not None:
                b_part3(*pb2)
            b_part3(*last2)

        def ln_chain():
            E("dve", lambda: vector.tensor_scalar_mul(out=mean[:], in0=ps[S1][:, :], scalar1=1.0 / D),
              reads=[("ps", S1)], writes=["mean"])
            im = ntf()
            E("dve", lambda im=im: vector.tensor_tensor(out=tmpf[im][:], in0=mean[:], in1=mean[:], op=ALU.mult),
              reads=["mean"], writes=[("tmpf", im)])
            E("dve", lambda im=im: vector.scalar_tensor_tensor(out=lnr[:], in0=ps[S2][:, :], scalar=1.0 / D,
                                                                in1=tmpf[im][:], op0=ALU.mult, op1=ALU.subtract),
              reads=[("ps", S2), ("tmpf", im)], writes=["lnr"])
            E("act", lambda: scalar.activation(out=lnr[:], in_=lnr[:], func=AF.Sqrt, bias=epsc[:, 0:1]),
              reads=["lnr", "epsc"], writes=["lnr"])
            E("dve", lambda: vector.reciprocal(out=lnr[:], in_=lnr[:]), reads=["lnr"], writes=["lnr"])
            for c in range(8):
                E("pool", lambda c=c: gpsimd.tensor_tensor(out=cb[:, c * T:(c + 1) * T], in0=cb[:, c * T:(c + 1) * T],
                                                           in1=mean[:], op=ALU.subtract),
                  reads=[("cb", c), "mean"], writes=[("cb", c)])
                E("pool", lambda c=c: gpsimd.tensor_tensor(out=cb[:, c * T:(c + 1) * T], in0=cb[:, c * T:(c + 1) * T],
                                                           in1=lnr[:], op=ALU.mult),
                  reads=[("cb", c), "lnr"], writes=[("cb", c)])

        def ln_silu():
            for c in range(8):
                E("act", lambda c=c: scalar.activation(out=zblk(ZB + c), in_=cb[:, c * T:(c + 1) * T], func=AF.Silu,
                                                       bias=prmcol(pb + 41, c), scale=prmcol(pb + 40, c)),
                  reads=[("cb", c), "prm"], writes=[("z", ZB + c)])

        for i in range(2):
            s, _ = stream.get(t, l, cbase + i)
            sk = slot_keys(s)
            for j in range(4):
                dc = 4 * i + j
                b = nbank()
                mm(b, [(ring[s][:, kc * 512 + j * P:kc * 512 + (j + 1) * P], hrhs(kc)) for kc in range(KC)],
                   reads=sk + hkeys)
                E("act", lambda dc=dc, b=b: scalar.activation(out=qm[:, dc * T:(dc + 1) * T], in_=ps[b][:, :],
                                                               func=AF.Copy),
                  reads=[("ps", b)], writes=[("qm", dc)])
                if dc == 3:
                    b_tail_stats()
                if dc == 4:
                    ln_chain()
        cbase += 2
        def att_s(hd):
            for mb in range(2):
                b = nbank()
                ip = 2 * (hd % 2) + mb
                mm(b, [(kT[l][:, (2 * hd + kk) * NMEM + mb * P:(2 * hd + kk) * NMEM + (mb + 1) * P],
                        qm[:, (2 * hd + kk) * T:(2 * hd + kk + 1) * T]) for kk in range(2)],
                   reads=[("kT", l), ("qm", 2 * hd), ("qm", 2 * hd + 1)])
                E("act", lambda ip=ip, b=b: scalar.activation(out=pT[ip][:], in_=ps[b][:, :], func=AF.Exp,
                                                               scale=1.0 / 16.0),
                  reads=[("ps", b)], writes=[("pT", ip)])

        def att_o(hd):
            ips = [2 * (hd % 2), 2 * (hd % 2) + 1]
            pk = [("pT", ip) for ip in ips]
            bd = nbank()
            mm(bd, [(onesb[:], pT[ip][:]) for ip in ips], reads=pk + ["onesb"])
            ir = ntf()
            E("dve", lambda ir=ir, bd=bd: vector.reciprocal(out=tmpf[ir][:], in_=ps[bd][:, :]),
              reads=[("ps", bd)], writes=[("tmpf", ir)])
            for dd in range(2):
                dc = 2 * hd + dd
                b = nbank()
                mm(b, [(vS[l][:, mb * D + dc * P:mb * D + (dc + 1) * P], pT[ips[mb]][:]) for mb in range(2)],
                   reads=pk + [("vS", l)])
                E("dve", lambda dc=dc, b=b, ir=ir: vector.tensor_tensor(out=zblk(ZC + dc), in0=ps[b][:, :],
                                                                         in1=tmpf[ir][:], op=ALU.mult),
                  reads=[("ps", b), ("tmpf", ir)], writes=[("z", ZC + dc)])

        if os.environ.get("ATT_PIPE", "1") == "1":
            att_s(0)
            for hd in range(4):
                if hd + 1 < 4:
                    att_s(hd + 1)
                att_o(hd)
        else:
            for hd in range(4):
                att_s(hd)
                att_o(hd)
        ln_silu()
        if D0:
            dbg("zabc", bufA, 24 * T, [("z", i) for i in range(24)])
            dbg("qT", qm, KC * T, [("qm", i) for i in range(8)])
            dbg("cb", cb, KC * T, [("cb", i) for i in range(8)])
        zk = {ZA: [("z", ZA + i) for i in range(8)], ZB: [("z", ZB + i) for i in range(8)],
              ZC: [("z", ZC + i) for i in range(8)]}
        for pair in range(4):
            accs = [None, None]
            for bi, zb in enumerate((ZA, ZB, ZC)):
                s, _ = stream.get(t, l, cbase + pair * 3 + bi)
                sk = slot_keys(s)
                for cc in range(2):
                    c = 2 * pair + cc
                    by, bgt = nbank(), nbank()
                    mm(by, [(ring[s][:, kc * 256 + cc * P:kc * 256 + (cc + 1) * P], zblk(zb + kc))
                            for kc in range(KC)], reads=sk + zk[zb])
                    mm(bgt, [(ring[s][:, 2048 + kc * 256 + cc * P:2048 + kc * 256 + (cc + 1) * P], hrhs(kc))
                             for kc in range(KC)], reads=sk + hkeys)
                    igt = ntf()
                    E("act", lambda bgt=bgt, igt=igt, bi=bi, c=c: scalar.activation(
                        out=tmpf[igt][:], in_=ps[bgt][:, :], func=AF.Sigmoid, bias=prmcol(pb + 2 + bi, c)),
                      reads=[("ps", bgt), "prm"], writes=[("tmpf", igt)])
                    E("dve", lambda by=by, igt=igt: vector.tensor_tensor(out=tmpf[igt][:], in0=ps[by][:, :],
                                                                          in1=tmpf[igt][:], op=ALU.mult),
                      reads=[("ps", by), ("tmpf", igt)], writes=[("tmpf", igt)])
                    if bi == 0:
                        accs[cc] = igt
                    elif bi == 1:
                        acc = accs[cc]
                        E("dve", lambda acc=acc, igt=igt: vector.tensor_tensor(out=tmpf[acc][:], in0=tmpf[acc][:],
                                                                                in1=tmpf[igt][:], op=ALU.add),
                          reads=[("tmpf", acc), ("tmpf", igt)], writes=[("tmpf", acc)])
                    else:
                        acc = accs[cc]
                        E("dve", lambda acc=acc, igt=igt, c=c: vector.tensor_tensor(
                            out=qm[:, c * T:(c + 1) * T], in0=tmpf[acc][:], in1=tmpf[igt][:], op=ALU.add),
                          reads=[("tmpf", acc), ("tmpf", igt)], writes=[("qm", c)])
        cbase += 12
        qk = [("qm", i) for i in range(8)]
        if D0:
            dbg("mT", qm, KC * T, qk)
        def wo_groups():
            for i in range(2):
                s, _ = stream.get(t, l, cbase + i)
                sk = slot_keys(s)
                for j in range(4):
                    prs = [(ring[s][:, kc * 512 + j * P:kc * 512 + (j + 1) * P], qm[:, kc * T:(kc + 1) * T])
                           for kc in range(KC)]
                    if i == 0 and j == 0:
                        yield (prs, sk, [[("qm", kc)] for kc in range(KC)])
                    else:
                        yield (prs, sk + qk)
        resid_phase(wo_groups())
        cbase += 2
        if D0:
            dbg("x1", xT, KC * T, xkeys)
        return cbase

    def ffn(t, l, cbase):
        pb = l * NPRM
        norm_to_h(pb + 42, have_stats=True)
        W2 = 2 + T
        for jj in range(11):
            s, _ = stream.get(t, l, cbase + jj)
            sk = slot_keys(s)
            for j2 in range(2):
                j = 2 * jj + j2
                o0 = j2 * 2048
                bgt, bup = nbank(), nbank()
                gtp = [(ring[s][:, kc * 256 + j2 * P:kc * 256 + (j2 + 1) * P], hrhs(kc)) for kc in range(KC)]
                upp = [(ring[s][:, 2048 + kc * 256 + j2 * P:2048 + kc * 256 + (j2 + 1) * P], hrhs(kc))
                       for kc in range(KC)]
                if j == 0:
                    mm(bgt, gtp, reads=sk, fine=[[("hT", kc)] for kc in range(KC)])
                else:
                    mm(bgt, gtp, reads=sk + hkeys)
                mm(bup, upp, reads=sk + hkeys)
                fi = rot("fGU", 2)
                hofs = (l * NJ + j) * 4
                fv = fGU[fi][:, :].rearrange("p (a w) -> p a w", w=W2)
                if False:
                    pass
                else:
                    E("act", lambda fi=fi, hofs=hofs, fv=fv: scalar.activation(
                        out=fv[:, :, 0:2], in_=hF[:, hofs:hofs + 4].rearrange("p (a w) -> p a w", w=2), func=AF.Copy),
                      reads=["hF"], writes=[("fGh", fi)])
                accs = []
                for hh, (bk, cbk) in enumerate(((bgt, j), (bup, NJ + j))):
                    ia = ntf()
                    accs.append(ia)
                    base = hh * W2
                    if hh == 0 and os.environ.get("F_DVECOPY", "0") == "1":
                        E("dve", lambda fi=fi, bk=bk, base=base: vector.tensor_copy(
                            out=fGU[fi][:, base + 2:base + 2 + T], in_=ps[bk][:, :]),
                          reads=[("ps", bk)], writes=[("fGm", fi, hh)])
                    else:
                        E("act", lambda fi=fi, bk=bk, base=base: scalar.activation(
                            out=fGU[fi][:, base + 2:base + 2 + T], in_=ps[bk][:, :], func=AF.Copy),
                          reads=[("ps", bk)], writes=[("fGm", fi, hh)])
                    E("act", lambda bk=bk, ia=ia, cbk=cbk: scalar.activation(
                        out=tmpf[ia][:], in_=ps[bk][:, :], func=AF.Identity, scale=fcol(l, 2, cbk)),
                      reads=[("ps", bk), "fcw"], writes=[("tmpf", ia)])
                    eng, h = "dve", vector
                    for k in (1, 0):
                        E(eng, lambda fi=fi, ia=ia, cbk=cbk, k=k, base=base, h=h: h.scalar_tensor_tensor(
                            out=tmpf[ia][:], in0=fGU[fi][:, base + k:base + k + T], scalar=fcol(l, k, cbk),
                            in1=tmpf[ia][:], op0=ALU.mult, op1=ALU.add),
                          reads=[("fGh", fi), ("fGm", fi, hh), ("tmpf", ia), "fcw"], writes=[("tmpf", ia)])
                if False:
                    pass
                else:
                    E("act", lambda hofs=hofs, bgt=bgt: scalar.activation(
                        out=hF[:, hofs:hofs + 2], in_=ps[bgt][:, T - 2:T], func=AF.Copy),
                      reads=[("ps", bgt)], writes=["hF"])
                    E("act", lambda hofs=hofs, bup=bup: scalar.activation(
                        out=hF[:, hofs + 2:hofs + 4], in_=ps[bup][:, T - 2:T], func=AF.Copy),
                      reads=[("ps", bup)], writes=["hF"])
                ig, iu = accs
                E("act", lambda ig=ig: scalar.activation(out=tmpf[ig][:], in_=tmpf[ig][:], func=AF.Silu),
                  reads=[("tmpf", ig)], writes=[("tmpf", ig)])
                fe, fh = ("pool", gpsimd) if (os.environ.get("F_POOL", "1") == "1" and t > 0) else ("dve", vector)
                E(fe, lambda j=j, ig=ig, iu=iu, fh=fh: fh.tensor_tensor(out=zblk(j), in0=tmpf[iu][:],
                                                                         in1=tmpf[ig][:], op=ALU.mult),
                  reads=[("tmpf", ig), ("tmpf", iu)], writes=[("z", j)])
        cbase += 11
        zk = [("z", j) for j in range(NJ)]
        if t == dbg_t and l == 0:
            dbg("zf", bufA, NJ * T, zk)
        NG, NE = 3, NJ - 2
        dslots = {}

        def dget(i):
            if i not in dslots:
                dslots[i] = stream.get(t, l, cbase + i, hold=True)
            return dslots[i]

        def wpairs(g):
            s, _ = dget(g // 2)
            jj = g % 2
            return s, [(ring[s][:, kc * 256 + jj * P:kc * 256 + (jj + 1) * P], zblk(kc)) for kc in range(NJ)]

        def resid_tail(dc, b, pend):
            E("dve", lambda: vector.tensor_tensor(out=xT[:, dc * T:(dc + 1) * T], in0=xT[:, dc * T:(dc + 1) * T],
                                                  in1=ps[b][:, :], op=ALU.add),
              reads=[("ps", b), ("xT", dc)], writes=[("xT", dc)])
            h = stat_sq(xT, T, dc, [("xT", dc)])
            if pend is not None:
                stat_mm(pend)
            return h

        obanks = {}
        for g in range(NG):
            s, prs = wpairs(g)
            b = nbank()
            obanks[g] = b
            for kc in range(NE):
                E("pe", lambda kc=kc, prs=prs, b=b: tensor.matmul(ps[b][:, :], lhsT=prs[kc][0], rhs=prs[kc][1],
                                                                  start=(kc == 0), stop=False),
                  reads=slot_keys(s) + [("z", kc)], writes=[("ps", b)])
        pend = None
        for g in range(8):
            s, prs = wpairs(g)
            if g < NG:
                b = obanks[g]
                for kc in range(NE, NJ):
                    E("pe", lambda kc=kc, prs=prs, b=b: tensor.matmul(ps[b][:, :], lhsT=prs[kc][0], rhs=prs[kc][1],
                                                                      start=False, stop=(kc == NJ - 1)),
                      reads=slot_keys(s) + [("z", kc)], writes=[("ps", b)])
            else:
                b = nbank()
                mm(b, prs, slot_keys(s) + zk)
            if g % 2 == 1:
                stream.unhold(dslots[g // 2][1])
            pend = resid_tail(g, b, pend)
        stat_mm(pend)
        cbase += 4
        if t == dbg_t and l == 0:
            dbg("x2", xT, KC * T, xkeys)
        return cbase

    def final_out(t):
        out_keys = []
        if do_final:
            rstd_finish(T)
            scale_rows(cb, "cb", 2 * NPRM)
            src, skey = cb, "cb"
            if t == dbg_t:
                dbg("yT", cb, KC * T, [("cb", i) for i in range(8)])
        else:
            src, skey = xT, "xT"
        for blk in range(4):
            yi = rot("yout", 2)
            for hb in range(2):
                b = nbank()

                def fn(blk=blk, hb=hb, b=b):
                    return [tensor.transpose(ps[b][:, i * P:(i + 1) * P],
                                             src[:, (4 * hb + i) * T + blk * P:(4 * hb + i) * T + (blk + 1) * P],
                                             ident[:]) for i in range(4)]
                E("pe", fn, reads=[(skey, 4 * hb + i) for i in range(4)] + ["ident"], writes=[("ps", b)])
                E("act", lambda yi=yi, hb=hb, b=b: scalar.activation(out=yout[yi][:, hb * 512:(hb + 1) * 512],
                                                                      in_=ps[b][:, :], func=AF.Copy),
                  reads=[("ps", b)], writes=[("yout", yi)])
            E("sp", lambda yi=yi, blk=blk: sync.dma_start(out=y_v[4 * t + blk], in_=yout[yi][:]),
              reads=[("yout", yi)], writes=[("y", 4 * t + blk)], dma=f"yout{yi}")
            out_keys.append(("y", 4 * t + blk))
        return out_keys

    dbg_keys = []

    def dbg(name, buf, n, keys):
        if not debug:
            return
        dt = buf.dtype if hasattr(buf, "dtype") else F32
        dd = nc.dram_tensor("dbg_" + name, [P, n], dt, kind="ExternalOutput").ap()
        E("sp", lambda: sync.dma_start(out=dd, in_=buf[:, :n]), reads=keys, writes=[("dbg", name)], dma="dbg")
        dbg_keys.append(("dbg", name))

    all_out = []
    for t in range(n_tiles):
        load_x_tile(t)
        for l in range(depth):
            cbase = mixer(t, l)
            cbase = ffn(t, l, cbase)
            assert cbase == len(chunks)
        all_out += final_out(t)
    trk.wait_all("sp", all_out + dbg_keys)
    global _LAST_TRK
    _LAST_TRK = trk
    return nc


_NC_CACHE = {}


def kernel(x, mem, norm_mix_g, norm_mem_g, w_in, b_gate, conv_a_w, w_a_out, conv_b_w, conv_b_bias,
           ln_b_g, ln_b_b, w_b_out, w_kv, w_att_out, w_o, norm_ffn_g, w_up, conv_ffn_w, w_down, norm_final_g):
    f = lambda a: np.ascontiguousarray(np.asarray(a, dtype=np.float32))
    x = f(x)
    mem = f(mem)
    shared = dict(norm_mix_g=f(norm_mix_g), norm_mem_g=f(norm_mem_g), w_in=f(w_in), b_gate=f(b_gate),
                  conv_a_w=f(conv_a_w), w_a_out=f(w_a_out), conv_b_w=f(conv_b_w), conv_b_bias=f(conv_b_bias),
                  ln_b_g=f(ln_b_g), ln_b_b=f(ln_b_b), w_b_out=f(w_b_out), w_kv=f(w_kv), w_att_out=f(w_att_out),
                  w_o=f(w_o), norm_ffn_g=f(norm_ffn_g), w_up=f(w_up), conv_ffn_w=f(conv_ffn_w), w_down=f(w_down),
                  norm_final_g=f(norm_final_g))
    nb = x.shape[0]
    if "nc" not in _NC_CACHE:
        _NC_CACHE["nc"] = build_program()
    nc = _NC_CACHE["nc"]
    in_maps = [dict(shared, x=x[b], mem=mem[b]) for b in range(nb)]
    res = run_bass_kernel_spmd(nc, in_maps, core_ids=list(range(nb)))
    return np.stack([r["y"] for r in res.results], axis=0)
```
